# Optimizing a Trainium2 kernel written in Bass

```python
import jax, jax.numpy as jnp
from jax import lax
import numpy as np

D_MODEL = 2048
BATCH = 8
SEQ = 2048
DEPTH = 2
DEC_BATCH = 128
DEC_SEQ = 4
PAST_LEN = 2048
PAGE_SIZE = 128

N_MIXERS = 2
N_ATTN_LAYERS = (DEPTH + 1) // 2
N_CONV_LAYERS = DEPTH // 2
HEAD_DIM = 128
HEADS_PER_GROUP = D_MODEL // (2 * HEAD_DIM)
WINDOWS = (128, 512, 2048)
DILATIONS = (1, 4, 16)
N_GROUPS = 3
BLOCK = 128
ATTN_WIDTH = HEADS_PER_GROUP * HEAD_DIM
QKV_WIDTH = 3 * N_GROUPS * ATTN_WIDTH
ATTN_SCALE = HEAD_DIM ** -0.5
CONV_WIDTH = 31
D_FF = 4 * D_MODEL
RMS_EPS = 1e-6
LN_EPS = 1e-5

kernel_name = 'dilated_window_conformer_hybrid_step'


def rms_norm(x, g):
    xf = x.astype(jnp.float32)
    y = xf * lax.rsqrt(jnp.mean(xf * xf, axis=-1, keepdims=True) + RMS_EPS)
    return (y * g.astype(jnp.float32)).astype(x.dtype)


def layer_norm(x, g, b):
    xf = x.astype(jnp.float32)
    mu = jnp.mean(xf, axis=-1, keepdims=True)
    var = jnp.mean(jnp.square(xf - mu), axis=-1, keepdims=True)
    y = (xf - mu) * lax.rsqrt(var + LN_EPS) * g.astype(jnp.float32) + b.astype(jnp.float32)
    return y.astype(x.dtype)


def alibi_slopes(g):
    h = HEADS_PER_GROUP
    base = 2.0 ** (-8.0 * np.arange(1, h + 1) / h)
    return jnp.asarray((base / DILATIONS[g]).astype(np.float32))


def dilated_window_prompt(q, k, v, g):
    dil = DILATIONS[g]
    n_steps = WINDOWS[g] // dil
    slopes = alibi_slopes(g)
    B, S, H, DH = q.shape
    L = S // dil
    def by_residue(a):
        return a.reshape(B, L, dil, H, DH).transpose(0, 2, 1, 3, 4)
    nb = -(-L // BLOCK)
    Lp = nb * BLOCK
    pad = ((0, 0), (0, 0), (0, Lp - L), (0, 0), (0, 0))
    qb, kb, vb = [jnp.pad(by_residue(a), pad).reshape(B, dil, nb, BLOCK, H, DH) for a in (q, k, v)]
    def with_prev(a):
        prev = jnp.pad(a, ((0, 0), (0, 0), (1, 0), (0, 0), (0, 0), (0, 0)))[:, :, :-1]
        return jnp.concatenate([prev, a], axis=3)
    kk, vv = with_prev(kb), with_prev(vb)
    s = jnp.einsum('brnqhd,brnkhd->brnhqk', qb, kk, preferred_element_type=jnp.float32) * ATTN_SCALE
    qi = jnp.arange(BLOCK)[:, None]
    kj = jnp.arange(2 * BLOCK)[None, :]
    steps = BLOCK + qi - kj
    key_idx = jnp.arange(nb)[:, None, None] * BLOCK - BLOCK + kj[None]
    valid = (steps >= 0) & (steps <= n_steps) & (key_idx >= 0)
    bias = -slopes[:, None, None] * (dil * steps).astype(jnp.float32)[None]
    s = jnp.where(valid[:, None], s + bias, -jnp.inf)
    lse = jax.nn.logsumexp(s, axis=-1)
    p = jnp.exp(s - lse[..., None])
    o = jnp.einsum('brnhqk,brnkhd->brnqhd', p.astype(v.dtype), vv, preferred_element_type=jnp.float32)
    o = o.reshape(B, dil, Lp, H, DH)[:, :, :L].transpose(0, 2, 1, 3, 4).reshape(B, S, H, DH)
    lse = lse.transpose(0, 1, 2, 4, 3).reshape(B, dil, Lp, H)[:, :, :L]
    lse = lse.transpose(0, 2, 1, 3).reshape(B, S, H)
    return o, lse


def dilated_window_sample(q, kv_new, kv_buf, g):
    dil = DILATIONS[g]
    n_steps = WINDOWS[g] // dil
    slopes = alibi_slopes(g)
    T = q.shape[1]
    Wb = kv_buf.shape[1]
    steps = jnp.arange(n_steps + 1)
    idx = Wb + jnp.arange(T)[:, None] - dil * steps[None, :]
    valid = idx >= 0
    from_buf = (idx < Wb)[None, :, :, None, None, None]
    g_buf = kv_buf[:, jnp.clip(idx, 0, Wb - 1)]
    g_new = kv_new[:, jnp.clip(idx - Wb, 0, T - 1)]
    kv = jnp.where(from_buf, g_buf, g_new)
    k, v = kv[:, :, :, 0], kv[:, :, :, 1]
    s = jnp.einsum('bthd,btshd->bhts', q, k, preferred_element_type=jnp.float32) * ATTN_SCALE
    bias = -slopes[:, None, None] * (dil * steps).astype(jnp.float32)[None, None, :]
    s = jnp.where(valid, s + bias, -jnp.inf)
    lse = jax.nn.logsumexp(s, axis=-1)
    p = jnp.exp(s - lse[..., None])
    o = jnp.einsum('bhts,btshd->bthd', p.astype(v.dtype), v, preferred_element_type=jnp.float32)
    return o, lse.transpose(0, 2, 1)


def project_qkv(h, w_qkv):
    B, S, _ = h.shape
    return (h @ w_qkv).reshape(B, S, 3, N_GROUPS, HEADS_PER_GROUP, HEAD_DIM)


def merge_groups(outs, lses, w_o, dtype):
    alpha = jax.nn.softmax(jnp.stack(lses, axis=0), axis=0)
    o = jnp.einsum('gbsh,gbshd->bshd', alpha, jnp.stack(outs, axis=0))
    B, S = o.shape[:2]
    return o.astype(dtype).reshape(B, S, ATTN_WIDTH) @ w_o


def attention_prompt(h, w_qkv, w_o):
    qkv = project_qkv(h, w_qkv)
    S = h.shape[1]
    outs, lses, new_kv = [], [], []
    for g in range(N_GROUPS):
        o, l = dilated_window_prompt(qkv[:, :, 0, g], qkv[:, :, 1, g], qkv[:, :, 2, g], g)
        outs.append(o)
        lses.append(l)
        keep = min(WINDOWS[g], S)
        new_kv.append(qkv[:, S - keep:, 1:, g])
    return merge_groups(outs, lses, w_o, h.dtype), new_kv


def attention_sample(h, bufs, w_qkv, w_o):
    qkv = project_qkv(h, w_qkv)
    outs, lses, new_kv = [], [], []
    for g in range(N_GROUPS):
        kv_new = qkv[:, :, 1:, g]
        o, l = dilated_window_sample(qkv[:, :, 0, g], kv_new, bufs[g], g)
        outs.append(o)
        lses.append(l)
        new_kv.append(kv_new)
    return merge_groups(outs, lses, w_o, h.dtype), new_kv


def conformer_conv(h, history, w_pw1, b_pw1, w_dw, b_dw, ln_g, ln_b, w_pw2, b_pw2):
    a, b = jnp.split(h @ w_pw1 + b_pw1, 2, axis=-1)
    u = a * jax.nn.sigmoid(b)
    ext = jnp.concatenate([history.astype(u.dtype), u], axis=1)
    z = lax.conv_general_dilated(ext, w_dw[:, None, :].astype(u.dtype), (1,), 'VALID',
                                 dimension_numbers=('NWC', 'WIO', 'NWC'),
                                 feature_group_count=u.shape[-1]) + b_dw
    z = jax.nn.silu(layer_norm(z, ln_g, ln_b))
    return z @ w_pw2 + b_pw2, ext[:, -(CONV_WIDTH - 1):]


def squared_relu_mlp(x, g, w_up, w_down):
    return jnp.square(jax.nn.relu(rms_norm(x, g) @ w_up)) @ w_down


def setup_inputs(seed: int = 0) -> dict:
    key = jax.random.key(seed)
    ks = jax.random.split(key, 24)
    D = D_MODEL
    def nrm(k, shape, scale):
        return jax.random.normal(k, shape, jnp.float32) * scale
    return {
        'x_prompt': nrm(ks[0], (BATCH, SEQ, D), 1.0),
        'x_sample': nrm(ks[1], (DEC_BATCH, DEC_SEQ, D), 1.0),
        'cache_kv_g0': nrm(ks[2], (N_ATTN_LAYERS, DEC_BATCH, min(WINDOWS[0], PAST_LEN), 2, HEADS_PER_GROUP, HEAD_DIM), 1.0),
        'cache_kv_g1': nrm(ks[3], (N_ATTN_LAYERS, DEC_BATCH, min(WINDOWS[1], PAST_LEN), 2, HEADS_PER_GROUP, HEAD_DIM), 1.0),
        'cache_kv_g2': nrm(ks[4], (N_ATTN_LAYERS, DEC_BATCH, min(WINDOWS[2], PAST_LEN), 2, HEADS_PER_GROUP, HEAD_DIM), 1.0),
        'state_conv': nrm(ks[5], (N_CONV_LAYERS, DEC_BATCH, CONV_WIDTH - 1, D), 0.5),
        'attn_norm': 1.0 + nrm(ks[6], (N_ATTN_LAYERS, D), 0.02),
        'w_qkv': nrm(ks[7], (N_ATTN_LAYERS, D, QKV_WIDTH), D ** -0.5),
        'w_o': nrm(ks[8], (N_ATTN_LAYERS, ATTN_WIDTH, D), ATTN_WIDTH ** -0.5),
        'conv_norm': 1.0 + nrm(ks[9], (N_CONV_LAYERS, D), 0.02),
        'w_pw1': nrm(ks[10], (N_CONV_LAYERS, D, 2 * D), D ** -0.5),
        'b_pw1': nrm(ks[11], (N_CONV_LAYERS, 2 * D), 0.02),
        'w_dw': nrm(ks[12], (N_CONV_LAYERS, CONV_WIDTH, D), CONV_WIDTH ** -0.5),
        'b_dw': nrm(ks[13], (N_CONV_LAYERS, D), 0.02),
        'conv_ln_g': 1.0 + nrm(ks[14], (N_CONV_LAYERS, D), 0.02),
        'conv_ln_b': nrm(ks[15], (N_CONV_LAYERS, D), 0.02),
        'w_pw2': nrm(ks[16], (N_CONV_LAYERS, D, D), D ** -0.5),
        'b_pw2': nrm(ks[17], (N_CONV_LAYERS, D), 0.02),
        'mlp_norm': 1.0 + nrm(ks[18], (DEPTH, D), 0.02),
        'w_up': nrm(ks[19], (DEPTH, D, D_FF), D ** -0.5),
        'w_down': nrm(ks[20], (DEPTH, D_FF, D), D_FF ** -0.5),
        'final_norm': 1.0 + nrm(ks[21], (D,), 0.02),
    }


def reference(x_prompt, x_sample, cache_kv_g0, cache_kv_g1, cache_kv_g2, state_conv,
              attn_norm, w_qkv, w_o, conv_norm, w_pw1, b_pw1, w_dw, b_dw, conv_ln_g, conv_ln_b,
              w_pw2, b_pw2, mlp_norm, w_up, w_down, final_norm):
    caches = (cache_kv_g0, cache_kv_g1, cache_kv_g2)
    xp, xs = x_prompt, x_sample
    kv_p = [[] for _ in range(N_GROUPS)]
    kv_s = [[] for _ in range(N_GROUPS)]
    conv_p, conv_s = [], []
    for i in range(DEPTH):
        j = i // N_MIXERS
        if i % N_MIXERS == 0:
            dp, new_p = attention_prompt(rms_norm(xp, attn_norm[j]), w_qkv[j], w_o[j])
            ds, new_s = attention_sample(rms_norm(xs, attn_norm[j]), [c[j] for c in caches], w_qkv[j], w_o[j])
            for g in range(N_GROUPS):
                kv_p[g].append(new_p[g])
                kv_s[g].append(new_s[g])
        else:
            conv_w = (w_pw1[j], b_pw1[j], w_dw[j], b_dw[j], conv_ln_g[j], conv_ln_b[j], w_pw2[j], b_pw2[j])
            zero_hist = jnp.zeros((xp.shape[0], CONV_WIDTH - 1, xp.shape[2]), xp.dtype)
            dp, sp = conformer_conv(rms_norm(xp, conv_norm[j]), zero_hist, *conv_w)
            ds, ss = conformer_conv(rms_norm(xs, conv_norm[j]), state_conv[j], *conv_w)
            conv_p.append(sp)
            conv_s.append(ss)
        xp = xp + dp
        xs = xs + ds
        xp = xp + squared_relu_mlp(xp, mlp_norm[i], w_up[i], w_down[i])
        xs = xs + squared_relu_mlp(xs, mlp_norm[i], w_up[i], w_down[i])
    y_prompt = rms_norm(xp, final_norm)
    y_sample = rms_norm(xs, final_norm)
    kv_g0_prompt = jnp.stack(kv_p[0], axis=0)
    kv_g1_prompt = jnp.stack(kv_p[1], axis=0)
    kv_g2_prompt = jnp.stack(kv_p[2], axis=0)
    conv_prompt = jnp.stack(conv_p, axis=0)
    kv_g0_sample = jnp.stack(kv_s[0], axis=0)
    kv_g1_sample = jnp.stack(kv_s[1], axis=0)
    kv_g2_sample = jnp.stack(kv_s[2], axis=0)
    conv_sample = jnp.stack(conv_s, axis=0)
    return (y_prompt, y_sample, kv_g0_prompt, kv_g1_prompt, kv_g2_prompt, conv_prompt,
            kv_g0_sample, kv_g1_sample, kv_g2_sample, conv_sample)
```

```python
import contextlib
import numpy as np
import concourse.bass as bass
import concourse.mybir as mybir
from concourse.bass_utils import run_bass_kernel_spmd

F32 = mybir.dt.float32
BF16 = mybir.dt.bfloat16
AF = mybir.ActivationFunctionType
ALU = mybir.AluOpType

NCORES = 8
D = 2048
SEQ = 2048
NSEQ_S = 16
TS = 4
NS = NSEQ_S * TS
TOK = SEQ + NS
HD = 128
NH = 8
DFF = 8192
CW = 31
SCALE = HD ** -0.5
RMS_EPS = 1e-6
LN_EPS = 1e-5
DIL = (1, 4, 16)
DIST = (1, 4, 16)
EBASE = (0, 2, 7)
NE = 24
ARENA_WORDS = 52992
PE_CONV = True
POOL_CHUNKS = ()

ENGS = ("tensor", "vector", "scalar", "gpsimd", "sync")
SEM_ROT = 16000


class Res:
    __slots__ = ("name", "w", "r", "excl")

    def __init__(self, name="", excl=False):
        self.name = name
        self.w = None
        self.r = {}
        self.excl = excl


class Prog:
    def __init__(self, nc):
        self.nc = nc
        self.q = {e: [] for e in ENGS}
        self.cnt = {e: 0 for e in ENGS}
        self.sem = {}
        self.nsem = 0
        for e in ENGS:
            self._newsem(e)
        self.seen = {e: {} for e in ENGS}
        self.dsems = []
        self.ninstr = 0
        self.ndma = 0

    def _alloc(self, name):
        self.nsem += 1
        return self.nc.alloc_semaphore(name=f"{name}_{self.nsem}")

    def _newsem(self, e):
        self.sem[e] = self._alloc("e" + e)
        self.cnt[e] = 0

    def dsem(self, name):
        s = [self._alloc("d" + name), 0]
        self.dsems.append(s)
        return s

    def _wait(self, eng, ev):
        sem, val, src = ev
        k = id(sem)
        if self.seen[eng].get(k, 0) >= val:
            return
        self.seen[eng][k] = val
        self.q[eng].append(lambda h, sem=sem, val=val: h.wait_ge(sem, val))

    def _deps(self, eng, reads, writes, pe_accum):
        for r in reads:
            if r.w is not None:
                self._wait(eng, r.w)
        for w in writes:
            if w.w is not None and not (pe_accum and w.w[2] == "tensor"):
                self._wait(eng, w.w)
            for ev in w.r.values():
                self._wait(eng, ev)

    def _post(self, ev, reads, writes):
        k = id(ev[0])
        for r in reads:
            old = r.r.get(k)
            if old is None or old[1] < ev[1]:
                r.r[k] = ev
        for w in writes:
            w.w = ev
            w.r = {}

    def op(self, eng, fn, reads=(), writes=(), pe_accum=False):
        if any(r.excl for r in reads):
            writes = list(writes) + [r for r in reads if r.excl]
            reads = [r for r in reads if not r.excl]
        self._deps(eng, reads, writes, pe_accum)
        if self.cnt[eng] >= SEM_ROT:
            self._newsem(eng)
        sem = self.sem[eng]
        self.cnt[eng] += 1
        self.q[eng].append(lambda h, fn=fn, sem=sem: fn(h).then_inc(sem, 1))
        ev = (sem, self.cnt[eng], eng)
        self._post(ev, reads, writes)
        self.ninstr += 1
        return ev

    def dma(self, eng, fn, ds, reads=(), writes=()):
        self._deps(eng, reads, writes, False)
        ds[1] += 16
        sh = ds[0]
        self.q[eng].append(lambda h, fn=fn, sh=sh: fn(h).then_inc(sh, 16))
        ev = (sh, ds[1], "dma")
        self._post(ev, reads, writes)
        self.ndma += 1
        return ev

    def barrier(self, engs):
        evs = [(self.sem[e], self.cnt[e], e) for e in ENGS if self.cnt[e] > 0]
        evs += [(s[0], s[1], "dma") for s in self.dsems if s[1] > 0]
        for e in engs:
            for ev in evs:
                self._wait(e, ev)

    def emit(self):
        with self.nc.Block() as block:
            @block.tensor
            def _(h):
                for f in self.q["tensor"]:
                    f(h)

            @block.vector
            def _(h):
                for f in self.q["vector"]:
                    f(h)

            @block.scalar
            def _(h):
                for f in self.q["scalar"]:
                    f(h)

            @block.gpsimd
            def _(h):
                for f in self.q["gpsimd"]:
                    f(h)

            @block.sync
            def _(h):
                for f in self.q["sync"]:
                    f(h)


def _tables():
    base = 2.0 ** (-(np.arange(NH) + 1.0))
    k = np.arange(128)[:, None]
    q = np.arange(128)[None, :]
    etab = np.zeros((NH, 128, NE, 128), np.float32)
    for g in range(3):
        dil = DIL[g]
        for dist in range(DIST[g] + 1):
            delta = 128 * dist + q - k
            ok = (delta >= 0) & (delta % dil == 0) & (delta // dil <= 128)
            steps = np.where(ok, delta // dil, 0).astype(np.float64)
            for h in range(NH):
                etab[h, :, EBASE[g] + dist, :] = np.where(ok, np.exp(-base[h] * steps), 0.0)
    enew = np.zeros((NS, 3, NH, NS), np.float32)
    for b in range(NSEQ_S):
        for t in range(TS):
            for t2 in range(TS):
                for h in range(NH):
                    if t2 <= t:
                        enew[b * TS + t2, 0, h, b * TS + t] = np.exp(-base[h] * (t - t2))
                    if t2 == t:
                        enew[b * TS + t2, 1, h, b * TS + t] = 1.0
                        enew[b * TS + t2, 2, h, b * TS + t] = 1.0
    ec = np.zeros((128, 2, NH, TS), np.float32)
    j = np.arange(128)
    for h in range(NH):
        for t in range(TS):
            s0 = 128 + t - j
            ec[:, 0, h, t] = np.where(j >= t, np.exp(-base[h] * s0), 0.0)
            ec[:, 1, h, t] = np.exp(-base[h] * (128 - j))
    ident = np.eye(128, dtype=np.float32)
    return etab.reshape(NH, 128, NE * 128), enew.reshape(NS, 3 * NH * NS), ec.reshape(128, 2 * NH * TS), ident


def build_program(stop=99):
    nc = bass.Bass("TRN2", target_bir_lowering=False)
    P = Prog(nc)

    def finish():
        P.barrier(list(ENGS))
        P.emit()
        es.close()
        return nc, P

    def din(name, shape):
        return nc.dram_tensor(name, list(shape), F32, kind="ExternalInput").ap()

    def dout(name, shape):
        return nc.dram_tensor(name, list(shape), F32, kind="ExternalOutput").ap()

    xp = din("xp", (SEQ, D))
    xs = din("xs", (NS, D))
    cache = [din("c0", (NSEQ_S, 128, 2 * NH * HD)), din("c1", (NSEQ_S, 512, 2 * NH * HD)),
             din("c2", (NSEQ_S, 2048, 2 * NH * HD))]
    sconv = din("sconv", (NSEQ_S, CW - 1, D))
    wqkv = din("wqkv", (D, 9216))
    wo = din("wo", (NH * HD, D))
    wpw1 = din("wpw1", (D, 2 * D))
    wpw2 = din("wpw2", (D, D))
    wup = din("wup", (2, D, DFF))
    wdown = din("wdown", (2, DFF, D))
    nvec = din("nvec", (6, D))
    cvin = din("cvin", (576, 128))
    etab = din("etab", (NH, 128, NE * 128))
    enew = din("enew", (NS, 3 * NH * NS))
    ecin = din("ecin", (128, 2 * NH * TS))
    identin = din("ident", (128, 128))

    yp = dout("yp", (SEQ, D))
    ys = dout("ys", (NS, D))
    kvp = [dout("kv0p", (128, 2 * NH * HD)), dout("kv1p", (512, 2 * NH * HD)), dout("kv2p", (2048, 2 * NH * HD))]
    convp = dout("convp", (CW - 1, D))
    kvs = [dout(f"kv{g}s", (NS, 2 * NH * HD)) for g in range(3)]
    convs = dout("convs", (NSEQ_S, CW - 1, D))
    oTd = nc.dram_tensor("oTd", [128, 8 * TOK], BF16, kind="Internal").ap()
    xpark = nc.dram_tensor("xpark", [9 * 128, D], F32, kind="Internal").ap()

    es = contextlib.ExitStack()
    arena = es.enter_context(nc.sbuf_tensor("arena", [128, ARENA_WORDS], F32))
    banks = [es.enter_context(nc.psum_tensor(f"bank{i}", [128, 512], F32)) for i in range(8)]
    bres = [Res(f"bank{i}", excl=True) for i in range(8)]

    def f32v(off, n):
        return arena[:, off:off + n]

    def bfv(off, nbf):
        assert nbf % 2 == 0
        return arena[:, off:off + nbf // 2].bitcast(BF16)

    def bank_bf(i):
        return banks[i][:].bitcast(BF16)

    def mm(out, lhsT, rhs, start, stop, reads, writes, acc=False, sgc=False):
        kw = dict(skip_group_check=True) if sgc else {}
        return P.op("tensor", lambda h: h.matmul(out, lhsT=lhsT, rhs=rhs, start=start, stop=stop, **kw),
                    reads=reads, writes=writes, pe_accum=acc)

    def tr(out, in_, ident, reads, writes, acc=False):
        return P.op("tensor", lambda h: h.transpose(out=out, in_=in_, identity=ident), reads=reads, writes=writes,
                    pe_accum=acc)

    def act(out, in_, func, reads, writes, **kw):
        return P.op("scalar", lambda h: h.activation(out=out, in_=in_, func=func, **kw), reads=reads, writes=writes)

    def dve(fn, reads, writes):
        return P.op("vector", fn, reads=reads, writes=writes)

    cp_flip = [0]

    def copy_any(out, in_, reads, writes):
        cp_flip[0] ^= 1
        if cp_flip[0]:
            return act(out, in_, AF.Copy, reads, writes)
        return dve(lambda h: h.tensor_copy(out=out, in_=in_), reads, writes)

    G0 = 0
    o_identb = G0
    o_identf = o_identb + 64
    o_onesb = o_identf + 128
    o_cvec = o_onesb + 64
    o_utail = o_cvec + 576
    o_ss = o_utail + 480
    o_rstd = o_ss + 32
    o_misc = o_rstd + 32
    G_END = 1536
    assert o_misc + 64 <= G_END
    RING = G_END
    RING_WORDS = 12288
    R_END = RING + RING_WORDS

    identb = bfv(o_identb, 128)
    identf = f32v(o_identf, 128)
    onesb = bfv(o_onesb, 128)
    cvec = f32v(o_cvec, 576)
    utail = f32v(o_utail, 480).rearrange("p (c k) -> p c k", c=16)
    ssv = f32v(o_ss, 32)
    rstdv = f32v(o_rstd, 32)
    r_const = Res("const")
    r_ss = Res("ss")
    r_utail = Res("utail")

    s_const = P.dsem("const")
    P.dma("sync", lambda h: h.dma_start(out=identf, in_=identin), s_const, writes=[r_const])
    dve(lambda h: h.tensor_copy(out=identb, in_=identf), [r_const], [r_const])
    dve(lambda h: h.memset(onesb, 1.0), [], [r_const])
    epsr = f32v(o_misc, 1)
    epsl = f32v(o_misc + 1, 1)
    dve(lambda h: h.memset(epsr, RMS_EPS), [], [r_const])
    dve(lambda h: h.memset(epsl, LN_EPS), [], [r_const])

    slabres = [Res(f"slab{i}") for i in range(3)]
    slabsem = [P.dsem(f"slab{i}") for i in range(3)]
    cbres = [Res(f"cb{i}") for i in range(12)]
    cbsem = [P.dsem(f"cb{i}") for i in range(12)]
    bank_rr = [0]

    def next_bank(choices):
        b = choices[bank_rr[0] % len(choices)]
        bank_rr[0] += 1
        return b

    def load_gvec(dst, row, res, sem):
        return P.dma("sync", lambda h: h.dma_start(out=dst, in_=nvec[row].partition_broadcast(128)), sem, writes=[res])

    O_OT = R_END
    O_HT = O_OT + 8448
    O_QT = O_HT + 16896
    O_KT = O_QT + 3168
    O_V = O_KT + 3168
    O_ST = O_V + 3264
    O_END2 = O_ST + 3072
    assert O_END2 <= ARENA_WORDS, O_END2
    NB1 = 17
    oT = bfv(O_OT, 8 * TOK).rearrange("p (s t) -> p s t", s=8)
    r_oT = [Res(f"oT{i}") for i in range(5)]
    hT_all = bfv(O_HT, 16 * TOK).rearrange("p (c t) -> p c t", c=16)
    r_hT = [Res(f"hT{b}") for b in range(NB1)]

    def rows17(blk):
        return 128 if blk < 16 else NS

    NXIN = 3
    o_xin = O_QT
    o_hb = o_xin + NXIN * 2048
    o_junk = o_hb + 2 * 1024
    o_gvec = o_junk + 1024
    assert o_gvec + 2048 <= O_END2
    xin = [f32v(o_xin + i * 2048, 2048) for i in range(NXIN)]
    r_xin = [Res(f"xin{i}") for i in range(NXIN)]
    s_xin = [P.dsem(f"xin{i}") for i in range(NXIN)]
    gvec = f32v(o_gvec, 2048)
    r_gvec = Res("gvec")
    s_gvec = P.dsem("gvec")
    load_gvec(gvec, 0, r_gvec, s_gvec)
    ev0 = dve(lambda h: h.memset(ssv, 0.0), [], [r_ss])
    r_ss1 = [Res(f"ss{b}") for b in range(NB1)]
    for r_ in r_ss1:
        r_.w = ev0

    def load_xin(blk):
        i = blk % NXIN
        src = xp[blk * 128:(blk + 1) * 128, :] if blk < 16 else xs
        rows = rows17(blk)
        P.dma("sync", lambda h: h.dma_start(out=xin[i][:rows], in_=src), s_xin[i], writes=[r_xin[i]])

    hb1 = [bfv(o_hb + i * 1024, 2048) for i in range(2)]
    r_hb1 = [Res("hb0"), Res("hb1")]
    junk1 = bfv(o_junk, 2048)
    r_junk1 = Res("junk")

    def norm_stats(blk, rows, xb, xr, junk, rjunk, rs=None):
        rs = rs or r_ss
        col = blk % 32
        act(junk[:rows], xb[:rows], AF.Square, [xr], [rjunk, rs], accum_out=ssv[:rows, col:col + 1])

    def norm_fin(c0, c1, rs=None):
        rs = rs or r_ss
        act(rstdv[:, c0:c1], ssv[:, c0:c1], AF.Sqrt, [rs, r_const], [rs], scale=1.0 / D, bias=epsr)
        dve(lambda h: h.reciprocal(out=rstdv[:, c0:c1], in_=rstdv[:, c0:c1]), [rs], [rs])

    def norm_one(blk, rows, xb, xr, gv, rgv, hT, hres, t0, hbs, rhbs, junk, rjunk, stats=True, rs=None):
        col = blk % 32
        if stats:
            norm_stats(blk, rows, xb, xr, junk, rjunk, rs)
            norm_fin(col, col + 1, rs)
        r_ss_ = rs or r_ss
        hbi = hbs[blk % 2]
        rh = rhbs[blk % 2]
        dve(lambda h: h.scalar_tensor_tensor(out=hbi[:rows], in0=xb[:rows], scalar=rstdv[:rows, col:col + 1],
                                             in1=gv[:rows], op0=ALU.mult, op1=ALU.mult), [xr, r_ss_, rgv], [rh])
        for half in range(2):
            bk = next_bank([6, 7])
            pb = bank_bf(bk)
            for j in range(8):
                c = half * 8 + j
                tr(pb[:, j * 128:j * 128 + rows], hbi[:rows, c * 128:(c + 1) * 128], identb[:rows, :rows],
                   [rh, r_const], [bres[bk]], acc=(j > 0))
            src = pb.rearrange("p (c t) -> p c t", c=8)[:, :, :rows]
            copy_any(hT[:, half * 8:(half + 1) * 8, t0:t0 + rows], src, [bres[bk]], [hres])

    load_xin(0)
    load_xin(1)
    for blk in range(NB1):
        if blk + 2 < NB1:
            load_xin(blk + 2)
        norm_one(blk, rows17(blk), xin[blk % NXIN], r_xin[blk % NXIN], gvec, r_gvec, hT_all, r_hT[blk], blk * 128,
                 hb1, r_hb1, junk1, r_junk1, rs=r_ss1[blk])

    P.barrier(["tensor", "vector", "scalar", "sync"])
    if stop == 1:
        return finish()

    SL2 = [bfv(RING + i * 3072, 6144).rearrange("p (k w c) -> p k w c", k=16, w=3) for i in range(2)]
    o_E = RING + 6144
    o_Pf = o_E + 1536
    o_Pb = o_Pf + 4 * 256
    o_kvst = o_Pb + 4 * 256
    o_Kb = o_kvst + 3 * 256
    o_lnd = o_Kb + 4 * 64
    o_rden = o_lnd + 512
    assert o_rden + 512 <= R_END, o_rden + 512
    Etile = bfv(o_E, NE * 128).rearrange("p (e q) -> p e q", e=NE)
    r_E = Res("E")
    s_E = P.dsem("E")
    NST = 4
    STB = [0, 1, 2, 7]
    Pf = [bfv(o_Pf + i * 256, 512) for i in range(NST)]
    r_Pf = [Res(f"Pf{i}") for i in range(NST)]
    Pb = [bfv(o_Pb + i * 256, 512) for i in range(NST)]
    r_Pb = [Res(f"Pb{i}") for i in range(NST)]
    NKV = 3
    kvst = [f32v(o_kvst + i * 256, 256) for i in range(NKV)]
    r_kvst = [Res(f"kvst{i}") for i in range(NKV)]
    s_kvst = [P.dsem(f"kvst{i}") for i in range(NKV)]
    NKB = 4
    Kb = [bfv(o_Kb + i * 64, 128) for i in range(NKB)]
    r_Kb = [Res(f"Kb{i}") for i in range(NKB)]
    lnd = f32v(o_lnd, 512)
    r_lnd = Res("lnd")
    rden = f32v(o_rden, 512)
    r_rden = Res("rden")
    qT = [bfv(O_QT + g * 1056, TOK) for g in range(3)]
    KT = [bfv(O_KT + g * 1056, TOK) for g in range(3)]
    Vg = [bfv(O_V + g * 1088, 17 * 128).rearrange("p (b d) -> p b d", b=17) for g in range(3)]
    r_qT = [Res(f"qT{g}") for g in range(3)]
    r_KT = [Res(f"KT{g}") for g in range(3)]
    r_V = [Res(f"V{g}") for g in range(3)]
    qTs = bfv(O_ST, 24 * NS).rearrange("p (a t) -> p a t", a=24)
    KTs = bfv(O_ST + 768, 24 * NS).rearrange("p (a t) -> p a t", a=24)
    Vs = bfv(O_ST + 1536, 24 * 128).rearrange("p (a d) -> p a d", a=24)
    r_stash = Res("stash")
    s_out = P.dsem("out_misc")

    kvst_i = [0]
    kb_i = [0]
    slab_i = [0]
    TT5 = [(0, 512), (512, 512), (1024, 512), (1536, 512), (2048, NS)]

    for hs in range(NH):
        for g in range(3):
            si = slab_i[0] % 2
            slab_i[0] += 1
            slab = SL2[si]
            c0 = g * 1024 + hs * 128
            for w_ in range(3):
                src = wqkv[:, w_ * 3072 + c0:w_ * 3072 + c0 + 128].rearrange("(kc p) c -> p kc c", p=128)
                P.dma("gpsimd", lambda h, slab=slab, src=src, w_=w_: h.dma_start(out=slab[:, :, w_, :], in_=src),
                      slabsem[si], writes=[slabres[si]])
            for (t0, tn) in TT5:
                bk = next_bank([0, 1, 2])
                tiles_r = [r_hT[b] for b in range(t0 // 128, min(17, (t0 + tn + 127) // 128))]
                for kc in range(16):
                    mm(banks[bk][:, :tn], slab[:, kc, 0, :], hT_all[:, kc, t0:t0 + tn], kc == 0, kc == 15,
                       [slabres[si]] + tiles_r, [bres[bk]], acc=(kc > 0))
                copy_any(qT[g][:, t0:t0 + tn], banks[bk][:, :tn], [bres[bk]], [r_qT[g]])
            pend_tr = []

            def emit_ktr(g, blk, kbi, rows, t0):
                tbk = 7 if (blk // 8) % 2 == 0 else 6
                pb = bank_bf(tbk)
                j = blk % 8
                tr(pb[:, j * 128:j * 128 + rows], Kb[kbi][:rows, :], identb[:rows, :rows], [r_Kb[kbi], r_const], [bres[tbk]],
                   acc=(j > 0))
                if j == 7 or blk == NB1 - 1:
                    b0 = (blk // 8) * 8
                    ncol = t0 + rows - b0 * 128
                    copy_any(KT[g][:, b0 * 128:b0 * 128 + ncol], pb[:, :ncol], [bres[tbk]], [r_KT[g]])

            for blk in range(NB1):
                rows = rows17(blk)
                t0 = blk * 128
                bk = next_bank([0, 1, 2])
                for kc in range(16):
                    mm(banks[bk][:rows, 0:256], hT_all[:, kc, t0:t0 + rows], slab[:, kc, 1:3, :], kc == 0, kc == 15,
                       [slabres[si], r_hT[blk]], [bres[bk]], acc=(kc > 0))
                if blk == 16:
                    dst = kvs[g].rearrange("t (w h d) -> t w h d", w=2, h=NH)[:, :, hs, :]
                elif g == 2:
                    dst = kvp[2].rearrange("t (w h d) -> t w h d", w=2, h=NH)[t0:t0 + 128, :, hs, :]
                elif g == 1 and blk >= 12:
                    dst = kvp[1].rearrange("t (w h d) -> t w h d", w=2, h=NH)[t0 - 1536:t0 - 1536 + 128, :, hs, :]
                elif g == 0 and blk == 15:
                    dst = kvp[0].rearrange("t (w h d) -> t w h d", w=2, h=NH)[:, :, hs, :]
                else:
                    dst = None
                if dst is not None:
                    ki = kvst_i[0] % NKV
                    kvst_i[0] += 1
                    act(kvst[ki][:rows], banks[bk][:rows, 0:256], AF.Copy, [bres[bk]], [r_kvst[ki]])
                    P.dma("sync", lambda h, dst=dst, ki=ki, rows=rows: h.dma_start(
                        out=dst, in_=kvst[ki][:rows].rearrange("p (w d) -> p w d", w=2)), s_kvst[ki], reads=[r_kvst[ki]])
                dve(lambda h, g=g, blk=blk, bk=bk, rows=rows: h.tensor_copy(out=Vg[g][:rows, blk, :], in_=banks[bk][:rows, 128:256]),
                    [bres[bk]], [r_V[g]])
                kbi = kb_i[0] % NKB
                kb_i[0] += 1
                dve(lambda h, kbi=kbi, bk=bk, rows=rows: h.tensor_copy(out=Kb[kbi][:rows], in_=banks[bk][:rows, 0:128]),
                    [bres[bk]], [r_Kb[kbi]])
                pend_tr.append((blk, kbi, rows, t0))
                if len(pend_tr) > 2:
                    emit_ktr(g, *pend_tr.pop(0))
            while pend_tr:
                emit_ktr(g, *pend_tr.pop(0))
            a = g * NH + hs
            dve(lambda h, g=g, a=a: h.tensor_copy(out=qTs[:, a, :], in_=qT[g][:, SEQ:TOK]), [r_qT[g]], [r_stash])
            dve(lambda h, g=g, a=a: h.tensor_copy(out=KTs[:, a, :], in_=KT[g][:, SEQ:TOK]), [r_KT[g]], [r_stash])
            dve(lambda h, g=g, a=a: h.tensor_copy(out=Vs[:NS, a, :], in_=Vg[g][:NS, 16, :]), [r_V[g]], [r_stash])

        P.dma("gpsimd", lambda h, hs=hs: h.dma_start(out=Etile, in_=etab[hs].rearrange("p (e q) -> p e q", e=NE)),
              s_E, writes=[r_E])
        for c in range(4):
            nb_ = 3 + (c % 2) * 2
            db_ = nb_ + 1
            first = [True, True]
            batches = []
            for g in range(3):
                for qb in range(4 * c, 4 * c + 4):
                    dmax = min(qb, DIST[g])
                    d0 = 0
                    while d0 <= dmax:
                        n = min(4, dmax - d0 + 1)
                        batches.append((g, qb, d0, n))
                        d0 += n
            nbt = len(batches)

            def emit_st(i):
                g, qb, d0, n = batches[i]
                sb = i % NST
                bs_ = STB[sb]
                for j in range(n):
                    kb = qb - (d0 + j)
                    mm(banks[bs_][:, j * 128:(j + 1) * 128], KT[g][:, kb * 128:(kb + 1) * 128], qT[g][:, qb * 128:(qb + 1) * 128],
                       True, True, [r_KT[g], r_qT[g]], [bres[bs_]], acc=(j > 0))
                act(Pf[sb][:, :n * 128], banks[bs_][:, :n * 128], AF.Exp, [bres[bs_]], [r_Pf[sb]], scale=SCALE)
                e0 = EBASE[g] + d0
                dve(lambda h, sb=sb, n=n, e0=e0: h.tensor_tensor(
                    out=Pb[sb][:, :n * 128].rearrange("p (e q) -> p e q", e=n),
                    in0=Pf[sb][:, :n * 128].rearrange("p (e q) -> p e q", e=n),
                    in1=Etile[:, e0:e0 + n, :], op=ALU.mult),
                    [r_Pf[sb], r_E], [r_Pb[sb]])

            def emit_pv(i):
                g, qb, d0, n = batches[i]
                sb = i % NST
                ql = (qb - 4 * c) * 128
                for j in range(n):
                    kb = qb - (d0 + j)
                    mm(banks[nb_][:, ql:ql + 128], Vg[g][:, kb, :], Pb[sb][:, j * 128:(j + 1) * 128], first[0], False,
                       [r_V[g], r_Pb[sb]], [bres[nb_]], acc=not first[0], sgc=True)
                    first[0] = False
                    mm(banks[db_][:, ql:ql + 128], onesb, Pb[sb][:, j * 128:(j + 1) * 128], first[1], False,
                       [r_const, r_Pb[sb]], [bres[db_]], acc=not first[1], sgc=True)
                    first[1] = False

            LOOK = 3
            for i in range(min(LOOK, nbt)):
                emit_st(i)
            for i in range(nbt):
                if i + LOOK < nbt:
                    emit_st(i + LOOK)
                emit_pv(i)
            act(lnd, banks[db_][:], AF.Ln, [bres[db_]], [r_lnd])
            act(rden, lnd, AF.Exp, [r_lnd], [r_rden], scale=-1.0)
            dve(lambda h, nb_=nb_, hs=hs, c=c: h.tensor_tensor(out=oT[:, hs, c * 512:(c + 1) * 512], in0=banks[nb_][:], in1=rden, op=ALU.mult),
                [bres[nb_], r_rden], [r_oT[c]])

    P.barrier(list(ENGS))
    if stop == 2:
        return finish()

    o3 = O_HT
    o_enew = o3
    o_ec = o_enew + 768
    o_ktc = o_ec + 64
    o_pf3 = o_ktc + 2 * 512
    o_pb3 = o_pf3 + 2 * 512
    enew_t = bfv(o_enew, 3 * NH * NS).rearrange("p (g x) -> p g x", g=3)
    ec_t = bfv(o_ec, 2 * NH * TS).rearrange("p (k x) -> p k x", k=2)
    r_tab3 = Res("tab3")
    s_tab3 = P.dsem("tab3")
    P.dma("gpsimd", lambda h: h.dma_start(out=enew_t[:NS], in_=enew.rearrange("p (g x) -> p g x", g=3)), s_tab3, writes=[r_tab3])
    P.dma("gpsimd", lambda h: h.dma_start(out=ec_t, in_=ecin.rearrange("p (k x) -> p k x", k=2)), s_tab3, writes=[r_tab3])
    KTc = [bfv(o_ktc + i * 512, 1024).rearrange("p (a r) -> p a r", a=NH) for i in range(2)]
    r_KTc = [Res("KTc0"), Res("KTc1")]
    Pf3 = [f32v(o_pf3 + i * 512, 512) for i in range(2)]
    r_Pf3 = [Res("Pf30"), Res("Pf31")]
    Pb3 = [bfv(o_pb3 + i * 256, 512) for i in range(2)]
    r_Pb3 = [Res("Pb30"), Res("Pb31")]
    cbuf = [bfv(RING + i * 1024, 2048) for i in range(12)]
    NUMB, DENB = 3, 4
    firsts = [True, True]

    def pv_s(col0, n, lhs_v, lhs_ones, rhs, reads):
        mm(banks[NUMB][:, col0:col0 + n], lhs_v, rhs, firsts[0], False, reads, [bres[NUMB]], acc=not firsts[0], sgc=True)
        firsts[0] = False
        mm(banks[DENB][:, col0:col0 + n], lhs_ones, rhs, firsts[1], False, reads + [r_const], [bres[DENB]], acc=not firsts[1], sgc=True)
        firsts[1] = False

    for g in range(3):
        sb = g % 2
        for hh in range(NH):
            a = g * NH + hh
            mm(banks[sb][:NS, hh * NS:(hh + 1) * NS], KTs[:, a, :], qTs[:, a, :], True, True, [r_stash], [bres[sb]], acc=(hh > 0))
        act(Pf3[sb][:NS, :], banks[sb][:NS, :], AF.Exp, [bres[sb]], [r_Pf3[sb]], scale=SCALE)
        dve(lambda h, sb=sb, g=g: h.tensor_tensor(out=Pb3[sb][:NS, :], in0=Pf3[sb][:NS, :], in1=enew_t[:NS, g, :], op=ALU.mult),
            [r_Pf3[sb], r_tab3], [r_Pb3[sb]])
        for hh in range(NH):
            a = g * NH + hh
            pv_s(hh * NS, NS, Vs[:NS, a, :], onesb[:NS, :], Pb3[sb][:NS, hh * NS:(hh + 1) * NS], [r_stash, r_Pb3[sb]])

    cb_i = [0]
    units = [(b, g) for b in range(NSEQ_S) for g in range(3)]

    def load_unit(u):
        b, g = units[u]
        ids = []
        nblk = 1 if g == 0 else TS
        for t in range(nblk):
            ci = cb_i[0] % 12
            cb_i[0] += 1
            if g == 0:
                src = cache[0][b]
            else:
                src = cache[g][b].rearrange("(j r) n -> r j n", r=DIL[g])[t]
            P.dma("gpsimd", lambda h, ci=ci, src=src: h.dma_start(out=cbuf[ci], in_=src), cbsem[ci], writes=[cbres[ci]])
            ids.append(ci)
        return ids

    pending = {0: load_unit(0), 1: load_unit(1)}
    kt_i = [0]
    for u, (b, g) in enumerate(units):
        if u + 2 < len(units):
            pending[u + 2] = load_unit(u + 2)
        ids = pending.pop(u)
        sb = u % 2
        nq = TS if g == 0 else 1
        for t, ci in enumerate(ids):
            ki = kt_i[0] % 2
            kt_i[0] += 1
            tb = 6 + ki
            pbk = bank_bf(tb)
            for hh in range(NH):
                tr(pbk[:, hh * 128:(hh + 1) * 128], cbuf[ci][:, hh * 128:(hh + 1) * 128], identb, [cbres[ci], r_const], [bres[tb]], acc=(hh > 0))
            copy_any(KTc[ki], pbk.rearrange("p (a r) -> p a r", a=NH), [bres[tb]], [r_KTc[ki]])
            for hh in range(NH):
                a = g * NH + hh
                if g == 0:
                    mm(banks[sb][:, hh * TS:(hh + 1) * TS], KTc[ki][:, hh, :], qTs[:, a, b * TS:(b + 1) * TS], True, True,
                       [r_KTc[ki], r_stash], [bres[sb]], acc=(hh > 0 or t > 0))
                else:
                    mm(banks[sb][:, hh * TS + t:hh * TS + t + 1], KTc[ki][:, hh, :], qTs[:, a, b * TS + t:b * TS + t + 1], True, True,
                       [r_KTc[ki], r_stash], [bres[sb]], acc=(hh > 0 or t > 0))
        act(Pf3[sb][:, :NH * TS], banks[sb][:, :NH * TS], AF.Exp, [bres[sb]], [r_Pf3[sb]], scale=SCALE)
        kind = 0 if g == 0 else 1
        dve(lambda h, sb=sb, kind=kind: h.tensor_tensor(out=Pb3[sb][:, :NH * TS], in0=Pf3[sb][:, :NH * TS], in1=ec_t[:, kind, :], op=ALU.mult),
            [r_Pf3[sb], r_tab3], [r_Pb3[sb]])
        for t, ci in enumerate(ids):
            for hh in range(NH):
                vv = cbuf[ci][:, 1024 + hh * 128:1024 + (hh + 1) * 128]
                if g == 0:
                    pv_s(hh * NS + b * TS, TS, vv, onesb, Pb3[sb][:, hh * TS:(hh + 1) * TS], [cbres[ci], r_Pb3[sb]])
                else:
                    pv_s(hh * NS + b * TS + t, 1, vv, onesb, Pb3[sb][:, hh * TS + t:hh * TS + t + 1], [cbres[ci], r_Pb3[sb]])
    dve(lambda h: h.reciprocal(out=rden, in_=banks[DENB][:]), [bres[DENB]], [r_rden])
    dve(lambda h: h.tensor_tensor(out=oT[:, :, SEQ:TOK], in0=banks[NUMB][:].rearrange("p (a t) -> p a t", a=NH),
                                  in1=rden.rearrange("p (a t) -> p a t", a=NH), op=ALU.mult),
        [bres[NUMB], r_rden], [r_oT[4]])

    s_oTd = P.dsem("oTd")
    P.dma("sync", lambda h: h.dma_start(out=oTd, in_=bfv(O_OT, 8 * TOK)), s_oTd, reads=r_oT)
    P.barrier(list(ENGS))
    if stop == 3:
        return finish()

    SL = [bfv(RING + i * 4096, 8192) for i in range(3)]
    slab_i[0] = 0

    def next_slab():
        si = slab_i[0] % 3
        slab_i[0] += 1
        return si

    O_X = R_END
    O_H = O_X + 9 * 2048
    O_A = O_H + 8704
    O_T = O_A + 4352
    assert O_T + 7680 <= ARENA_WORDS
    xres_ = f32v(O_X, 9 * 2048).rearrange("p (b d) -> p b d", b=9)
    s_xld = P.dsem("xld")
    xb_sems = [P.dsem(f"xb{i}") for i in range(9)]
    s_gv = P.dsem("gv")
    s_y = [P.dsem("y0"), P.dsem("y1")]
    s_park = P.dsem("park")
    s_stq = [P.dsem("stq0"), P.dsem("stq1")]
    s_co = P.dsem("convout")
    s_cv = P.dsem("cvin")

    o_cvl = O_T
    r_cv = Res("cvec")
    r_cvl = Res("cvl")
    cvl_t = f32v(O_T, 640).rearrange("p (a c) -> p a c", a=5)
    for a_ in range(5):
        nr = min(128, 576 - a_ * 128)
        P.dma("sync", lambda h, a_=a_, nr=nr: h.dma_start(out=cvl_t[:nr, a_, :], in_=cvin[a_ * 128:a_ * 128 + nr, :]), s_cv, writes=[r_cvl])
    for a_ in range(5):
        nr = min(128, 576 - a_ * 128)
        bk = next_bank([0, 1, 2, 3, 4, 5])
        tr(banks[bk][:, :nr], cvl_t[:nr, a_, :], identf[:nr, :nr], [r_cvl, r_const], [bres[bk]])
        dve(lambda h, a_=a_, nr=nr, bk=bk: h.tensor_copy(out=cvec[:, a_ * 128:a_ * 128 + nr], in_=banks[bk][:, :nr]), [bres[bk]], [r_cv])
    CV_BPW1, CV_BDW, CV_LNG, CV_LNB, CV_WDW = 0, 32, 48, 64, 80
    P.barrier(["tensor", "vector", "scalar", "sync"])

    ALLB = [0, 1, 2, 3, 4, 5, 6, 7]
    r_x_all = [Res(f"x{b}") for b in range(9)]

    def load_x_block(pass_, blk):
        rows = NS if (pass_ == 1 and blk == 8) else 128
        src = xs if (pass_ == 1 and blk == 8) else xp[pass_ * 1024 + blk * 128:pass_ * 1024 + (blk + 1) * 128, :]
        P.dma("sync", lambda h: h.dma_start(out=xres_[:rows, blk, :], in_=src), xb_sems[blk], writes=[r_x_all[blk]])

    for ps_ in range(2):
        nblk = 8 if ps_ == 0 else 9
        tokbase = 0 if ps_ == 0 else 1024
        T = 1024 if ps_ == 0 else 1024 + NS

        def rows_p(blk):
            return NS if (ps_ == 1 and blk == 8) else 128

        tiles = [(0, 512), (512, 512)] + ([(1024, NS)] if ps_ == 1 else [])
        r_x = r_x_all
        hT = bfv(O_H, 16 * 1088).rearrange("p (c t) -> p c t", c=16)
        r_h = [Res(f"h{b}") for b in range(nblk)]
        abuf = [bfv(O_A + i * 2176, 4 * 1088).rearrange("p (m t) -> p m t", m=4) for i in range(2)]
        r_a = [Res("a0"), Res("a1")]

        s_xb = xb_sems
        if ps_ == 0:
            for blk in range(nblk):
                load_x_block(0, blk)
        oTp = bfv(O_A, 8 * 1088).rearrange("p (s t) -> p s t", s=8)
        r_oTp = Res("oTp")
        P.dma("sync", lambda h, tb=tokbase, T=T: h.dma_start(
            out=oTp[:, :, :T], in_=oTd.rearrange("p (s t) -> p s t", s=8)[:, :, tb:tb + T]), s_oTd, writes=[r_oTp])
        wo_sl = []
        for e2 in range(2):
            si = next_slab()
            wv = SL[si].rearrange("p (e s n) -> p e s n", e=2, s=8)
            for e_ in range(2):
                dsl = e2 * 2 + e_
                P.dma("gpsimd", lambda h, wv=wv, e_=e_, dsl=dsl: h.dma_start(
                    out=wv[:, e_, :, :], in_=wo[:, dsl * 512:(dsl + 1) * 512].rearrange("(s p) n -> p s n", p=128)),
                    slabsem[si], writes=[slabres[si]])
                wo_sl.append((wv[:, e_, :, :], slabres[si]))
        for blk in range(nblk):
            rows = rows_p(blk)
            for dsl in range(4):
                wv_, wres = wo_sl[dsl]
                bk = next_bank(ALLB)
                for s_ in range(NH):
                    mm(banks[bk][:rows, :], oTp[:, s_, blk * 128:blk * 128 + rows], wv_[:, s_, :], s_ == 0, s_ == NH - 1,
                       [r_oTp, wres], [bres[bk]], acc=(s_ > 0))
                xv = xres_[:rows, blk, dsl * 512:(dsl + 1) * 512]
                dve(lambda h, xv=xv, bk=bk, rows=rows: h.tensor_tensor(out=xv, in0=xv, in1=banks[bk][:rows, :], op=ALU.add),
                    [bres[bk], r_x[blk]], [r_x[blk]])
        P.barrier(["tensor", "vector", "scalar", "sync"])

        def do_norm(row):
            hbs = [bfv(O_T + i * 1024, 2048) for i in range(2)]
            rhbs = [Res("hb0"), Res("hb1")]
            junk = bfv(O_T + 2048, 2048)
            rjunk = Res("junk")
            gv = f32v(O_T + 3072, 2048)
            rgv = Res("gv")
            load_gvec(gv, row, rgv, s_gv)
            dve(lambda h: h.memset(ssv, 0.0), [r_ss], [r_ss])
            for blk in range(nblk):
                norm_stats(blk, rows_p(blk), xres_[:, blk, :], r_x[blk], junk, rjunk)
            norm_fin(0, nblk)
            for blk in range(nblk):
                norm_one(blk, rows_p(blk), xres_[:, blk, :], r_x[blk], gv, rgv, hT, r_h[blk], blk * 128, hbs, rhbs, junk, rjunk,
                         stats=False)

        def blocks_of(t0, tn):
            return [r_h[b] for b in range(t0 // 128, (t0 + tn + 127) // 128)]

        def do_mlp(i):
            rt = [f32v(O_T + 5120 + k * 512, 512) for k in range(3)]
            r_rt = [Res(f"rt{k}") for k in range(3)]
            rt_i = [0]

            def up(s):
                si = next_slab()
                wv = SL[si].rearrange("p (k n) -> p k n", k=16)
                P.dma("gpsimd", lambda h, wv=wv, s=s: h.dma_start(
                    out=wv, in_=wup[i][:, s * 512:(s + 1) * 512].rearrange("(kc p) n -> p kc n", p=128)), slabsem[si], writes=[slabres[si]])
                ab = abuf[s % 2]
                for m in range(4):
                    for (t0, tn) in tiles:
                        bk = next_bank(ALLB)
                        rr = blocks_of(t0, tn)
                        for kc in range(16):
                            mm(banks[bk][:, :tn], wv[:, kc, m * 128:(m + 1) * 128], hT[:, kc, t0:t0 + tn], kc == 0, kc == 15,
                               [slabres[si]] + rr, [bres[bk]], acc=(kc > 0))
                        k = rt_i[0] % 3
                        rt_i[0] += 1
                        act(rt[k][:, :tn], banks[bk][:, :tn], AF.Relu, [bres[bk]], [r_rt[k]])
                        act(ab[:, m, t0:t0 + tn], rt[k][:, :tn], AF.Square, [r_rt[k]], [r_a[s % 2]])

            def down(s):
                si = next_slab()
                wv = SL[si].rearrange("p (m n) -> p m n", m=4)
                P.dma("gpsimd", lambda h, wv=wv, s=s: h.dma_start(
                    out=wv, in_=wdown[i][s * 512:(s + 1) * 512, :].rearrange("(m p) n -> p m n", p=128)), slabsem[si], writes=[slabres[si]])
                ab = abuf[s % 2]
                for blk in range(nblk):
                    rows = rows_p(blk)
                    for dsl in range(4):
                        bk = next_bank(ALLB)
                        for m in range(4):
                            mm(banks[bk][:rows, :], ab[:, m, blk * 128:blk * 128 + rows], wv[:, m, dsl * 512:(dsl + 1) * 512],
                               m == 0, m == 3, [r_a[s % 2], slabres[si]], [bres[bk]], acc=(m > 0))
                        xv = xres_[:rows, blk, dsl * 512:(dsl + 1) * 512]
                        dve(lambda h, xv=xv, bk=bk, rows=rows: h.tensor_tensor(out=xv, in0=xv, in1=banks[bk][:rows, :], op=ALU.add),
                            [bres[bk], r_x[blk]], [r_x[blk]])

            up(0)
            for s in range(16):
                if s + 1 < 16:
                    up(s + 1)
                down(s)

        do_norm(1)
        do_mlp(0)
        P.barrier(["tensor", "vector", "scalar", "sync"])

        do_norm(2)
        for blk in range(nblk):
            rows = rows_p(blk)
            P.dma("sync", lambda h, blk=blk, rows=rows: h.dma_start(out=xpark[blk * 128:blk * 128 + rows, :], in_=xres_[:rows, blk, :]),
                  s_park, reads=[r_x[blk]])
        P.barrier(["tensor", "vector", "scalar", "sync"])
        zT = f32v(O_X, 16 * 1088).rearrange("p (c t) -> p c t", c=16)
        r_z = [Res(f"z{j}") for j in range(16)]
        XW = 1056
        o_ext = O_A
        NEXT = 2
        ext = [f32v(o_ext + i * XW, XW) for i in range(NEXT)]
        r_ext = [Res(f"ext{i}") for i in range(NEXT)]
        o_exb = o_ext + NEXT * XW
        exb = [bfv(o_exb + i * 528, 1056) for i in range(2)]
        r_exb = [Res("exb0"), Res("exb1")]
        o_dg = o_exb + 2 * 528
        dg = bfv(o_dg, CW * 128).rearrange("p (k c) -> p k c", k=CW)
        r_dg = [Res(f"dg{k}") for k in range(CW)]
        o_sig = o_dg + 1984
        sig = [f32v(o_sig, 512) for i in range(2)]
        r_sig = [Res("sig0")] * 2
        o_exts = o_sig + 512
        exts = [f32v(o_exts + i * 544, 544).rearrange("p (b k) -> p b k", b=NSEQ_S) for i in range(2)]
        r_exts = [Res("exts0"), Res("exts1")]
        o_stq = o_exts + 1088
        stq = [f32v(o_stq + i * 512, 512).rearrange("p (a c) -> p a c", a=4) for i in range(2)]
        r_stq = [Res("stq0"), Res("stq1")]
        o_cpo = o_stq + 1024
        cpo = f32v(o_cpo, 2048)
        r_cpo = Res("cpo")
        cso = f32v(o_cpo + 2048, 2048)
        r_cso = Res("cso")
        usc = [f32v(o_cpo + 4096 + i * 64, 64) for i in range(2)]
        r_usc = [Res("usc0"), Res("usc1")]
        assert o_cpo + 4096 + 128 <= ARENA_WORDS
        sig_i = [0]
        if ps_ == 1:
            P.dma("sync", lambda h: h.dma_start(out=convs[:, 0:26, :], in_=sconv[:, 4:30, :]), s_co)
        for sl in range(8):
            si = next_slab()
            wv = SL[si].rearrange("p (k w c) -> p k w c", k=16, w=2)
            for w_ in range(2):
                P.dma("gpsimd", lambda h, wv=wv, sl=sl, w_=w_: h.dma_start(
                    out=wv[:, :, w_, :], in_=wpw1[:, w_ * D + sl * 256:w_ * D + (sl + 1) * 256].rearrange("(kc p) c -> p kc c", p=128)),
                    slabsem[si], writes=[slabres[si]])
            for jj in range(2):
                j = sl * 2 + jj
                ei = j % NEXT
                ex = ext[ei]
                rex = r_ext[ei]
                if ps_ == 0:
                    dve(lambda h, ex=ex: h.memset(ex[:, 0:30], 0.0), [], [rex])
                else:
                    dve(lambda h, ex=ex, j=j: h.tensor_copy(out=ex[:, 0:30], in_=utail[:, j, :]), [r_utail], [rex])
                for (t0, tn) in tiles:
                    bka = next_bank(ALLB)
                    bkb = next_bank(ALLB)
                    rr = blocks_of(t0, tn)
                    for kc in range(16):
                        mm(banks[bka][:, :tn], wv[:, kc, 0, jj * 128:(jj + 1) * 128], hT[:, kc, t0:t0 + tn], kc == 0, kc == 15,
                           [slabres[si]] + rr, [bres[bka]], acc=(kc > 0))
                    for kc in range(16):
                        mm(banks[bkb][:, :tn], wv[:, kc, 1, jj * 128:(jj + 1) * 128], hT[:, kc, t0:t0 + tn], kc == 0, kc == 15,
                           [slabres[si]] + rr, [bres[bkb]], acc=(kc > 0))
                    k = sig_i[0] % 2
                    sig_i[0] += 1
                    act(sig[k][:, :tn], banks[bkb][:, :tn], AF.Sigmoid, [bres[bkb], r_cv], [r_sig[k]],
                        bias=cvec[:, CV_BPW1 + 16 + j:CV_BPW1 + 16 + j + 1])
                    if t0 < 1024:
                        uo = ex[:, 30 + t0:30 + t0 + tn]
                        wr = [rex]
                    else:
                        uo = usc[j % 2]
                        wr = [r_usc[j % 2]]
                    if t0 < 1024:
                        dve(lambda h, uo=uo, bka=bka, tn=tn, k=k, j=j: h.scalar_tensor_tensor(
                            out=uo, in0=banks[bka][:, :tn], scalar=cvec[:, CV_BPW1 + j:CV_BPW1 + j + 1], in1=sig[k][:, :tn],
                            op0=ALU.add, op1=ALU.mult), [bres[bka], r_sig[k], r_cv], wr)
                    else:
                        dve(lambda h, uo=uo, bka=bka, tn=tn, k=k, j=j: h.scalar_tensor_tensor(
                            out=uo, in0=banks[bka][:, :tn], scalar=cvec[:, CV_BPW1 + j:CV_BPW1 + j + 1], in1=sig[k][:, :tn],
                            op0=ALU.add, op1=ALU.mult), [bres[bka], r_sig[k], r_cv], wr)
                zj = zT[:, j, 0:1024]
                if PE_CONV:
                    xb_ = exb[j % 2]
                    rxb = r_exb[j % 2]
                    act(xb_[:, 0:1054], ex[:, 0:1054], AF.Copy, [rex], [rxb])
                    for k_ in range(CW):
                        col = CV_WDW + k_ * 16 + j
                        act(dg[:, k_, :], identf, AF.Copy, [r_const, r_cv], [r_dg[k_]], scale=cvec[:, col:col + 1])
                    for (t0, tn) in tiles[:2]:
                        bk = next_bank(ALLB)
                        for k_ in range(CW):
                            mm(banks[bk][:, :512], dg[:, k_, :], xb_[:, k_ + t0:k_ + t0 + 512], k_ == 0, k_ == CW - 1,
                               [r_dg[k_], rxb], [bres[bk]], acc=(k_ > 0))
                        act(zT[:, j, t0:t0 + 512], banks[bk][:, :512], AF.Identity, [bres[bk], r_cv], [r_z[j]],
                            bias=cvec[:, CV_BDW + j:CV_BDW + j + 1])
                ceng = "gpsimd" if j in POOL_CHUNKS else "vector"
                if not PE_CONV:
                    P.op(ceng, lambda h, zj=zj, ex=ex, j=j: h.tensor_scalar(
                        out=zj, in0=ex[:, 0:1024], scalar1=cvec[:, CV_WDW + j:CV_WDW + j + 1], scalar2=cvec[:, CV_BDW + j:CV_BDW + j + 1],
                        op0=ALU.mult, op1=ALU.add), reads=[rex, r_cv], writes=[r_z[j]])
                for k_ in range(1, CW if not PE_CONV else 1):
                    P.op(ceng, lambda h, zj=zj, ex=ex, j=j, k_=k_: h.scalar_tensor_tensor(
                        out=zj, in0=ex[:, k_:k_ + 1024], scalar=cvec[:, CV_WDW + k_ * 16 + j:CV_WDW + k_ * 16 + j + 1], in1=zj,
                        op0=ALU.mult, op1=ALU.add), reads=[rex, r_cv, r_z[j]], writes=[r_z[j]])
                if ps_ == 0:
                    dve(lambda h, ex=ex, j=j: h.tensor_copy(out=utail[:, j, :], in_=ex[:, 1024:1054]), [rex], [r_utail])
                else:
                    bk = next_bank(ALLB)
                    tr(banks[bk][:30, 0:128], ex[:, 1024:1054], identf, [rex, r_const], [bres[bk]])
                    copy_any(cpo[:30, j * 128:(j + 1) * 128], banks[bk][:30, 0:128], [bres[bk]], [r_cpo])
                    qi = j % 2
                    P.dma("sync", lambda h, qi=qi, j=j: h.dma_start(
                        out=stq[qi][:120], in_=sconv[:, :, j * 128:(j + 1) * 128].rearrange("(a b) k c -> (b k) a c", a=4)),
                        s_stq[qi], writes=[r_stq[qi]])
                    bk = next_bank(ALLB)
                    for a_ in range(4):
                        tr(banks[bk][:, a_ * 120:(a_ + 1) * 120], stq[qi][:120, a_, :], identf[:120, :120], [r_stq[qi], r_const], [bres[bk]],
                           acc=(a_ > 0))
                    exj = exts[j % 2]
                    rexs = r_exts[j % 2]
                    copy_any(exj[:, :, 0:30], banks[bk][:, 0:480].rearrange("p (b k) -> p b k", b=NSEQ_S), [bres[bk]], [rexs])
                    dve(lambda h, exj=exj, j=j: h.tensor_copy(out=exj[:, :, 30:34], in_=usc[j % 2].rearrange("p (b t) -> p b t", b=NSEQ_S)),
                        [r_usc[j % 2], rexs], [rexs])
                    zs = zT[:, j, 1024:1088].rearrange("p (b t) -> p b t", b=NSEQ_S)
                    dve(lambda h, zs=zs, exj=exj, j=j: h.tensor_scalar(
                        out=zs, in0=exj[:, :, 0:4], scalar1=cvec[:, CV_WDW + j:CV_WDW + j + 1], scalar2=cvec[:, CV_BDW + j:CV_BDW + j + 1],
                        op0=ALU.mult, op1=ALU.add), [rexs, r_cv], [r_z[j]])
                    for k_ in range(1, CW):
                        dve(lambda h, zs=zs, exj=exj, j=j, k_=k_: h.scalar_tensor_tensor(
                            out=zs, in0=exj[:, :, k_:k_ + 4], scalar=cvec[:, CV_WDW + k_ * 16 + j:CV_WDW + k_ * 16 + j + 1], in1=zs,
                            op0=ALU.mult, op1=ALU.add), [rexs, r_cv, r_z[j]], [r_z[j]])
                    bk = next_bank(ALLB)
                    tr(banks[bk][:NS, 0:128], usc[j % 2], identf, [r_usc[j % 2], r_const], [bres[bk]])
                    copy_any(cso[:NS, j * 128:(j + 1) * 128], banks[bk][:NS, 0:128], [bres[bk]], [r_cso])
        if ps_ == 1:
            P.dma("sync", lambda h: h.dma_start(out=convp, in_=cpo[:30, :]), s_co, reads=[r_cpo])
            for b_ in range(NSEQ_S):
                P.dma("sync", lambda h, b_=b_: h.dma_start(out=convs[b_, 26:30, :], in_=cso[b_ * TS:(b_ + 1) * TS, :]), s_co, reads=[r_cso])
        P.barrier(["tensor", "vector", "scalar", "sync"])

        onesf = f32v(O_A, 128)
        r_onesf = Res("onesf")
        dve(lambda h: h.memset(onesf, 1.0 / D), [], [r_onesf])
        o_sq = O_A + 128
        sq = [f32v(o_sq + i * 1088, 1088) for i in range(2)]
        r_sq = [Res("sq0"), Res("sq1")]
        rsd = f32v(o_sq + 2176, 1088)
        r_rsd = Res("rsd")
        assert o_sq + 3 * 1088 <= O_T
        mb = [0, 1, 2]
        vb = [3, 4, 5]
        for ti, (t0, tn) in enumerate(tiles):
            for j in range(16):
                mm(banks[mb[ti]][:, :tn], onesf, zT[:, j, t0:t0 + tn], j == 0, j == 15, [r_onesf, r_z[j]], [bres[mb[ti]]], acc=(j > 0))
        for j in range(16):
            for ti, (t0, tn) in enumerate(tiles):
                zv = zT[:, j, t0:t0 + tn]
                dve(lambda h, zv=zv, ti=ti, tn=tn: h.tensor_tensor(out=zv, in0=zv, in1=banks[mb[ti]][:, :tn], op=ALU.subtract),
                    [bres[mb[ti]], r_z[j]], [r_z[j]])
            act(sq[j % 2][:, :T], zT[:, j, :T], AF.Square, [r_z[j]], [r_sq[j % 2]])
            for ti, (t0, tn) in enumerate(tiles):
                mm(banks[vb[ti]][:, :tn], onesf, sq[j % 2][:, t0:t0 + tn], j == 0, j == 15, [r_onesf, r_sq[j % 2]], [bres[vb[ti]]], acc=(j > 0))
        for ti, (t0, tn) in enumerate(tiles):
            act(rsd[:, t0:t0 + tn], banks[vb[ti]][:, :tn], AF.Sqrt, [bres[vb[ti]], r_const], [r_rsd], bias=epsl)
            dve(lambda h, t0=t0, tn=tn: h.reciprocal(out=rsd[:, t0:t0 + tn], in_=rsd[:, t0:t0 + tn]), [r_rsd], [r_rsd])
        sT = hT
        for j in range(16):
            dve(lambda h, j=j, T=T: h.tensor_tensor(out=zT[:, j, :T], in0=zT[:, j, :T], in1=rsd[:, :T], op=ALU.mult), [r_z[j], r_rsd], [r_z[j]])
            act(sT[:, j, :T], zT[:, j, :T], AF.Silu, [r_z[j], r_cv], [r_h[b] for b in range(nblk)],
                scale=cvec[:, CV_LNG + j:CV_LNG + j + 1], bias=cvec[:, CV_LNB + j:CV_LNB + j + 1])
        P.barrier(["tensor", "vector", "scalar", "sync"])

        bb = f32v(O_T, 2048)
        r_bb = Res("bb")
        load_gvec(bb, 5, r_bb, s_gv)
        for blk in range(nblk):
            rows = rows_p(blk)
            P.dma("sync", lambda h, blk=blk, rows=rows: h.dma_start(out=xres_[:rows, blk, :], in_=xpark[blk * 128:blk * 128 + rows, :]),
                  s_xld, writes=[r_x[blk]])
        for blk in range(nblk):
            r_x[blk].w = (s_xld[0], s_xld[1], "dma")
        for blk in range(nblk):
            rows = rows_p(blk)
            dve(lambda h, blk=blk, rows=rows: h.tensor_tensor(out=xres_[:rows, blk, :], in0=xres_[:rows, blk, :], in1=bb[:rows], op=ALU.add),
                [r_x[blk], r_bb], [r_x[blk]])
        for s in range(4):
            si = next_slab()
            wv = SL[si].rearrange("p (m n) -> p m n", m=4)
            P.dma("gpsimd", lambda h, wv=wv, s=s: h.dma_start(
                out=wv, in_=wpw2[s * 512:(s + 1) * 512, :].rearrange("(m p) n -> p m n", p=128)), slabsem[si], writes=[slabres[si]])
            for blk in range(nblk):
                rows = rows_p(blk)
                for dsl in range(4):
                    bk = next_bank(ALLB)
                    for m in range(4):
                        mm(banks[bk][:rows, :], sT[:, s * 4 + m, blk * 128:blk * 128 + rows], wv[:, m, dsl * 512:(dsl + 1) * 512],
                           m == 0, m == 3, [r_h[blk], slabres[si]], [bres[bk]], acc=(m > 0))
                    xv = xres_[:rows, blk, dsl * 512:(dsl + 1) * 512]
                    dve(lambda h, xv=xv, bk=bk, rows=rows: h.tensor_tensor(out=xv, in0=xv, in1=banks[bk][:rows, :], op=ALU.add),
                        [bres[bk], r_x[blk]], [r_x[blk]])
        P.barrier(["tensor", "vector", "scalar", "sync"])

        do_norm(3)
        do_mlp(1)
        P.barrier(["tensor", "vector", "scalar", "sync"])

        gv = f32v(O_T, 2048)
        rgv = Res("gvf")
        load_gvec(gv, 4, rgv, s_gv)
        yst = [f32v(O_T + 2048 + i * 2048, 2048) for i in range(2)]
        r_yst = [Res("yst0"), Res("yst1")]
        junk = bfv(O_T + 6144, 2048)
        rjunk = Res("junkf")
        dve(lambda h: h.memset(ssv, 0.0), [r_ss], [r_ss])
        for blk in range(nblk):
            norm_stats(blk, rows_p(blk), xres_[:, blk, :], r_x[blk], junk, rjunk)
        norm_fin(0, nblk)
        for blk in range(nblk):
            rows = rows_p(blk)
            xb = xres_[:, blk, :]
            col = blk
            yi = blk % 2
            dve(lambda h, rows=rows, col=col, xb=xb, yi=yi: h.scalar_tensor_tensor(
                out=yst[yi][:rows], in0=xb[:rows], scalar=rstdv[:rows, col:col + 1], in1=gv[:rows], op0=ALU.mult, op1=ALU.mult),
                [r_x[blk], r_ss, rgv], [r_yst[yi]])
            if ps_ == 1 and blk == 8:
                dst = ys
            else:
                dst = yp[tokbase + blk * 128:tokbase + (blk + 1) * 128, :]
            P.dma("sync", lambda h, dst=dst, yi=yi, rows=rows: h.dma_start(out=dst, in_=yst[yi][:rows]), s_y[yi], reads=[r_yst[yi]])
            if ps_ == 0:
                load_x_block(1, blk)
                if blk == nblk - 1:
                    load_x_block(1, 8)
        P.barrier(["tensor", "vector", "scalar", "sync"])

    return finish()


_CACHE = {}


def _prep_inputs(inp):
    etab, enew, ec, ident = _tables()
    f = lambda a: np.ascontiguousarray(np.asarray(a, dtype=np.float32))
    nvec = f(np.stack([inp["attn_norm"][0], inp["mlp_norm"][0], inp["conv_norm"][0], inp["mlp_norm"][1],
                       inp["final_norm"], inp["b_pw2"][0]], axis=0))
    cvin = f(np.concatenate([np.asarray(inp["b_pw1"][0]).reshape(32, 128), np.asarray(inp["b_dw"][0]).reshape(16, 128),
                             np.asarray(inp["conv_ln_g"][0]).reshape(16, 128), np.asarray(inp["conv_ln_b"][0]).reshape(16, 128),
                             np.asarray(inp["w_dw"][0]).reshape(CW * 16, 128)], axis=0))
    shared = dict(wqkv=f(inp["w_qkv"][0]), wo=f(inp["w_o"][0]), wpw1=f(inp["w_pw1"][0]), wpw2=f(inp["w_pw2"][0]),
                  wup=f(inp["w_up"]), wdown=f(inp["w_down"]), nvec=nvec, cvin=cvin, etab=etab, enew=enew, ecin=ec, ident=ident)
    maps = []
    for c in range(NCORES):
        sl = slice(c * NSEQ_S, (c + 1) * NSEQ_S)
        m = dict(shared)
        m["xp"] = f(inp["x_prompt"][c])
        m["xs"] = f(np.asarray(inp["x_sample"][sl]).reshape(NS, D))
        m["c0"] = f(np.asarray(inp["cache_kv_g0"][0, sl]).reshape(NSEQ_S, 128, 2048))
        m["c1"] = f(np.asarray(inp["cache_kv_g1"][0, sl]).reshape(NSEQ_S, 512, 2048))
        m["c2"] = f(np.asarray(inp["cache_kv_g2"][0, sl]).reshape(NSEQ_S, 2048, 2048))
        m["sconv"] = f(inp["state_conv"][0, sl])
        maps.append(m)
    return maps


def kernel(**inputs):
    if "nc" not in _CACHE:
        _CACHE["nc"] = build_program()[0]
    nc = _CACHE["nc"]
    maps = _prep_inputs(inputs)
    res = run_bass_kernel_spmd(nc, maps, core_ids=list(range(NCORES)))
    R = res.results
    cat = lambda k: np.stack([np.asarray(r[k], dtype=np.float32) for r in R], axis=0)
    y_prompt = cat("yp")
    y_sample = cat("ys").reshape(NCORES * NSEQ_S, TS, D)
    kvp_o = [cat(f"kv{g}p").reshape(NCORES, -1, 2, NH, HD)[None] for g in range(3)]
    conv_p = cat("convp")[None]
    kvs_o = [cat(f"kv{g}s").reshape(NCORES * NSEQ_S, TS, 2, NH, HD)[None] for g in range(3)]
    conv_s = cat("convs").reshape(NCORES * NSEQ_S, CW - 1, D)[None]
    return (y_prompt, y_sample, kvp_o[0], kvp_o[1], kvp_o[2], conv_p, kvs_o[0], kvs_o[1], kvs_o[2], conv_s)
```

```python
import contextlib
import numpy as np
import concourse.bass as bass
import concourse.mybir as mybir
from concourse.bass_utils import run_bass_kernel_spmd

F32 = mybir.dt.float32
BF16 = mybir.dt.bfloat16
AF = mybir.ActivationFunctionType
ALU = mybir.AluOpType

NCORES = 8
D = 2048
SEQ = 2048
NSEQ_S = 16
TS = 4
NS = NSEQ_S * TS
TOK = SEQ + NS
HD = 128
NH = 8
DFF = 8192
CW = 31
SCALE = HD ** -0.5
RMS_EPS = 1e-6
LN_EPS = 1e-5
DIL = (1, 4, 16)
DIST = (1, 4, 16)
EBASE = (0, 2, 7)
NE = 24
ARENA_WORDS = 52992
PE_CONV = True
POOL_CHUNKS = ()

ENGS = ("tensor", "vector", "scalar", "gpsimd", "sync")
SEM_ROT = 16000


class Res:
    __slots__ = ("name", "w", "r", "excl")

    def __init__(self, name="", excl=False):
        self.name = name
        self.w = None
        self.r = {}
        self.excl = excl


class Prog:
    def __init__(self, nc):
        self.nc = nc
        self.q = {e: [] for e in ENGS}
        self.cnt = {e: 0 for e in ENGS}
        self.sem = {}
        self.nsem = 0
        for e in ENGS:
            self._newsem(e)
        self.seen = {e: {} for e in ENGS}
        self.dsems = []
        self.ninstr = 0
        self.ndma = 0

    def _alloc(self, name):
        self.nsem += 1
        return self.nc.alloc_semaphore(name=f"{name}_{self.nsem}")

    def _newsem(self, e):
        self.sem[e] = self._alloc("e" + e)
        self.cnt[e] = 0

    def dsem(self, name):
        s = [self._alloc("d" + name), 0]
        self.dsems.append(s)
        return s

    def _wait(self, eng, ev):
        sem, val, src = ev
        k = id(sem)
        if self.seen[eng].get(k, 0) >= val:
            return
        self.seen[eng][k] = val
        self.q[eng].append(lambda h, sem=sem, val=val: h.wait_ge(sem, val))

    def _deps(self, eng, reads, writes, pe_accum):
        for r in reads:
            if r.w is not None:
                self._wait(eng, r.w)
        for w in writes:
            if w.w is not None and not (pe_accum and w.w[2] == "tensor"):
                self._wait(eng, w.w)
            for ev in w.r.values():
                self._wait(eng, ev)

    def _post(self, ev, reads, writes):
        k = id(ev[0])
        for r in reads:
            old = r.r.get(k)
            if old is None or old[1] < ev[1]:
                r.r[k] = ev
        for w in writes:
            w.w = ev
            w.r = {}

    def op(self, eng, fn, reads=(), writes=(), pe_accum=False):
        if any(r.excl for r in reads):
            writes = list(writes) + [r for r in reads if r.excl]
            reads = [r for r in reads if not r.excl]
        self._deps(eng, reads, writes, pe_accum)
        if self.cnt[eng] >= SEM_ROT:
            self._newsem(eng)
        sem = self.sem[eng]
        self.cnt[eng] += 1
        self.q[eng].append(lambda h, fn=fn, sem=sem: fn(h).then_inc(sem, 1))
        ev = (sem, self.cnt[eng], eng)
        self._post(ev, reads, writes)
        self.ninstr += 1
        return ev

    def dma(self, eng, fn, ds, reads=(), writes=()):
        self._deps(eng, reads, writes, False)
        ds[1] += 16
        sh = ds[0]
        self.q[eng].append(lambda h, fn=fn, sh=sh: fn(h).then_inc(sh, 16))
        ev = (sh, ds[1], "dma")
        self._post(ev, reads, writes)
        self.ndma += 1
        return ev

    def barrier(self, engs, skip=()):
        evs = [(self.sem[e], self.cnt[e], e) for e in ENGS if self.cnt[e] > 0]
        evs += [(s[0], s[1], "dma") for s in self.dsems if s[1] > 0 and not any(s is k for k in skip)]
        for e in engs:
            for ev in evs:
                self._wait(e, ev)

    def emit(self):
        with self.nc.Block() as block:
            @block.tensor
            def _(h):
                for f in self.q["tensor"]:
                    f(h)

            @block.vector
            def _(h):
                for f in self.q["vector"]:
                    f(h)

            @block.scalar
            def _(h):
                for f in self.q["scalar"]:
                    f(h)

            @block.gpsimd
            def _(h):
                for f in self.q["gpsimd"]:
                    f(h)

            @block.sync
            def _(h):
                for f in self.q["sync"]:
                    f(h)


def _tables():
    base = 2.0 ** (-(np.arange(NH) + 1.0))
    k = np.arange(128)[:, None]
    q = np.arange(128)[None, :]
    etab = np.zeros((NH, 128, NE, 128), np.float32)
    for g in range(3):
        dil = DIL[g]
        for dist in range(DIST[g] + 1):
            delta = 128 * dist + q - k
            ok = (delta >= 0) & (delta % dil == 0) & (delta // dil <= 128)
            steps = np.where(ok, delta // dil, 0).astype(np.float64)
            for h in range(NH):
                etab[h, :, EBASE[g] + dist, :] = np.where(ok, np.exp(-base[h] * steps), 0.0)
    enew = np.zeros((NS, 3, NH, NS), np.float32)
    for b in range(NSEQ_S):
        for t in range(TS):
            for t2 in range(TS):
                for h in range(NH):
                    if t2 <= t:
                        enew[b * TS + t2, 0, h, b * TS + t] = np.exp(-base[h] * (t - t2))
                    if t2 == t:
                        enew[b * TS + t2, 1, h, b * TS + t] = 1.0
                        enew[b * TS + t2, 2, h, b * TS + t] = 1.0
    ec = np.zeros((128, 2, NH, TS), np.float32)
    j = np.arange(128)
    for h in range(NH):
        for t in range(TS):
            s0 = 128 + t - j
            ec[:, 0, h, t] = np.where(j >= t, np.exp(-base[h] * s0), 0.0)
            ec[:, 1, h, t] = np.exp(-base[h] * (128 - j))
    ident = np.eye(128, dtype=np.float32)
    return etab.reshape(NH, 128, NE * 128), enew.reshape(NS, 3 * NH * NS), ec.reshape(128, 2 * NH * TS), ident


def build_program(stop=99):
    nc = bass.Bass("TRN2", target_bir_lowering=False)
    P = Prog(nc)

    def finish():
        P.barrier(list(ENGS))
        P.emit()
        es.close()
        return nc, P

    def din(name, shape):
        return nc.dram_tensor(name, list(shape), F32, kind="ExternalInput").ap()

    def dout(name, shape):
        return nc.dram_tensor(name, list(shape), F32, kind="ExternalOutput").ap()

    xp = din("xp", (SEQ, D))
    xs = din("xs", (NS, D))
    cache = [din("c0", (NSEQ_S, 128, 2 * NH * HD)), din("c1", (NSEQ_S, 512, 2 * NH * HD)),
             din("c2", (NSEQ_S, 2048, 2 * NH * HD))]
    sconv = din("sconv", (NSEQ_S, CW - 1, D))
    wqkv = din("wqkv", (D, 9216))
    wo = din("wo", (NH * HD, D))
    wpw1 = din("wpw1", (D, 2 * D))
    wpw2 = din("wpw2", (D, D))
    wup = din("wup", (2, D, DFF))
    wdown = din("wdown", (2, DFF, D))
    nvec = din("nvec", (6, D))
    cvin = din("cvin", (576, 128))
    etab = din("etab", (NH, 128, NE * 128))
    enew = din("enew", (NS, 3 * NH * NS))
    ecin = din("ecin", (128, 2 * NH * TS))
    identin = din("ident", (128, 128))

    yp = dout("yp", (SEQ, D))
    ys = dout("ys", (NS, D))
    kvp = [dout("kv0p", (128, 2 * NH * HD)), dout("kv1p", (512, 2 * NH * HD)), dout("kv2p", (2048, 2 * NH * HD))]
    convp = dout("convp", (CW - 1, D))
    kvs = [dout(f"kv{g}s", (NS, 2 * NH * HD)) for g in range(3)]
    convs = dout("convs", (NSEQ_S, CW - 1, D))
    oTd = nc.dram_tensor("oTd", [128, 8 * TOK], BF16, kind="Internal").ap()
    xpark = nc.dram_tensor("xpark", [9 * 128, D], F32, kind="Internal").ap()

    es = contextlib.ExitStack()
    arena = es.enter_context(nc.sbuf_tensor("arena", [128, ARENA_WORDS], F32))
    banks = [es.enter_context(nc.psum_tensor(f"bank{i}", [128, 512], F32)) for i in range(8)]
    bres = [Res(f"bank{i}", excl=True) for i in range(8)]

    def f32v(off, n):
        return arena[:, off:off + n]

    def bfv(off, nbf):
        assert nbf % 2 == 0
        return arena[:, off:off + nbf // 2].bitcast(BF16)

    def bank_bf(i):
        return banks[i][:].bitcast(BF16)

    def mm(out, lhsT, rhs, start, stop, reads, writes, acc=False, sgc=False):
        kw = dict(skip_group_check=True) if sgc else {}
        return P.op("tensor", lambda h: h.matmul(out, lhsT=lhsT, rhs=rhs, start=start, stop=stop, **kw),
                    reads=reads, writes=writes, pe_accum=acc)

    def tr(out, in_, ident, reads, writes, acc=False):
        return P.op("tensor", lambda h: h.transpose(out=out, in_=in_, identity=ident), reads=reads, writes=writes,
                    pe_accum=acc)

    def act(out, in_, func, reads, writes, **kw):
        return P.op("scalar", lambda h: h.activation(out=out, in_=in_, func=func, **kw), reads=reads, writes=writes)

    def dve(fn, reads, writes):
        return P.op("vector", fn, reads=reads, writes=writes)

    cp_flip = [0]

    def copy_any(out, in_, reads, writes):
        cp_flip[0] ^= 1
        if cp_flip[0]:
            return act(out, in_, AF.Copy, reads, writes)
        return dve(lambda h: h.tensor_copy(out=out, in_=in_), reads, writes)

    G0 = 0
    o_identb = G0
    o_identf = o_identb + 64
    o_onesb = o_identf + 128
    o_cvec = o_onesb + 64
    o_utail = o_cvec + 576
    o_ss = o_utail + 480
    o_rstd = o_ss + 32
    o_misc = o_rstd + 32
    G_END = 1536
    assert o_misc + 64 <= G_END
    RING = G_END
    RING_WORDS = 12288
    R_END = RING + RING_WORDS

    identb = bfv(o_identb, 128)
    identf = f32v(o_identf, 128)
    onesb = bfv(o_onesb, 128)
    cvec = f32v(o_cvec, 576)
    utail = f32v(o_utail, 480).rearrange("p (c k) -> p c k", c=16)
    ssv = f32v(o_ss, 32)
    rstdv = f32v(o_rstd, 32)
    r_const = Res("const")
    r_ss = Res("ss")
    r_utail = Res("utail")

    s_const = P.dsem("const")
    P.dma("sync", lambda h: h.dma_start(out=identf, in_=identin), s_const, writes=[r_const])
    dve(lambda h: h.tensor_copy(out=identb, in_=identf), [r_const], [r_const])
    dve(lambda h: h.memset(onesb, 1.0), [], [r_const])
    epsr = f32v(o_misc, 1)
    epsl = f32v(o_misc + 1, 1)
    dve(lambda h: h.memset(epsr, RMS_EPS), [], [r_const])
    dve(lambda h: h.memset(epsl, LN_EPS), [], [r_const])

    slabres = [Res(f"slab{i}") for i in range(3)]
    slabsem = [P.dsem(f"slab{i}") for i in range(3)]
    cbres = [Res(f"cb{i}") for i in range(12)]
    cbsem = [P.dsem(f"cb{i}") for i in range(12)]
    bank_rr = [0]

    def next_bank(choices):
        b = choices[bank_rr[0] % len(choices)]
        bank_rr[0] += 1
        return b

    def load_gvec(dst, row, res, sem):
        return P.dma("sync", lambda h: h.dma_start(out=dst, in_=nvec[row].partition_broadcast(128)), sem, writes=[res])

    O_OT = R_END
    O_HT = O_OT + 8448
    O_QT = O_HT + 16896
    O_KT = O_QT + 3168
    O_V = O_KT + 3168
    O_ST = O_V + 3264
    O_END2 = O_ST + 3072
    assert O_END2 <= ARENA_WORDS, O_END2
    NB1 = 17
    oT = bfv(O_OT, 8 * TOK).rearrange("p (s t) -> p s t", s=8)
    r_oT = [Res(f"oT{i}") for i in range(5)]
    hT_all = bfv(O_HT, 16 * TOK).rearrange("p (c t) -> p c t", c=16)
    r_hT = [Res(f"hT{b}") for b in range(NB1)]

    def rows17(blk):
        return 128 if blk < 16 else NS

    NXIN = 3
    o_xin = O_QT
    o_hb = o_xin + NXIN * 2048
    o_junk = o_hb + 2 * 1024
    o_gvec = o_junk + 1024
    assert o_gvec + 2048 <= O_END2
    xin = [f32v(o_xin + i * 2048, 2048) for i in range(NXIN)]
    r_xin = [Res(f"xin{i}") for i in range(NXIN)]
    s_xin = [P.dsem(f"xin{i}") for i in range(NXIN)]
    gvec = f32v(o_gvec, 2048)
    r_gvec = Res("gvec")
    s_gvec = P.dsem("gvec")
    load_gvec(gvec, 0, r_gvec, s_gvec)
    ev0 = dve(lambda h: h.memset(ssv, 0.0), [], [r_ss])
    r_ss1 = [Res(f"ss{b}") for b in range(NB1)]
    for r_ in r_ss1:
        r_.w = ev0

    def load_xin(blk):
        i = blk % NXIN
        src = xp[blk * 128:(blk + 1) * 128, :] if blk < 16 else xs
        rows = rows17(blk)
        P.dma("sync", lambda h: h.dma_start(out=xin[i][:rows], in_=src), s_xin[i], writes=[r_xin[i]])

    hb1 = [bfv(o_hb + i * 1024, 2048) for i in range(2)]
    r_hb1 = [Res("hb0"), Res("hb1")]
    junk1 = bfv(o_junk, 2048)
    r_junk1 = Res("junk")

    def norm_stats(blk, rows, xb, xr, junk, rjunk, rs=None):
        rs = rs or r_ss
        col = blk % 32
        act(junk[:rows], xb[:rows], AF.Square, [xr], [rjunk, rs], accum_out=ssv[:rows, col:col + 1])

    def norm_fin(c0, c1, rs=None):
        rs = rs or r_ss
        act(rstdv[:, c0:c1], ssv[:, c0:c1], AF.Sqrt, [rs, r_const], [rs], scale=1.0 / D, bias=epsr)
        dve(lambda h: h.reciprocal(out=rstdv[:, c0:c1], in_=rstdv[:, c0:c1]), [rs], [rs])

    def norm_one(blk, rows, xb, xr, gv, rgv, hT, hres, t0, hbs, rhbs, junk, rjunk, stats=True, rs=None):
        col = blk % 32
        if stats:
            norm_stats(blk, rows, xb, xr, junk, rjunk, rs)
            norm_fin(col, col + 1, rs)
        r_ss_ = rs or r_ss
        hbi = hbs[blk % 2]
        rh = rhbs[blk % 2]
        dve(lambda h: h.scalar_tensor_tensor(out=hbi[:rows], in0=xb[:rows], scalar=rstdv[:rows, col:col + 1],
                                             in1=gv[:rows], op0=ALU.mult, op1=ALU.mult), [xr, r_ss_, rgv], [rh])
        for half in range(2):
            bk = next_bank([6, 7])
            pb = bank_bf(bk)
            for j in range(8):
                c = half * 8 + j
                tr(pb[:, j * 128:j * 128 + rows], hbi[:rows, c * 128:(c + 1) * 128], identb[:rows, :rows],
                   [rh, r_const], [bres[bk]], acc=(j > 0))
            src = pb.rearrange("p (c t) -> p c t", c=8)[:, :, :rows]
            copy_any(hT[:, half * 8:(half + 1) * 8, t0:t0 + rows], src, [bres[bk]], [hres])

    load_xin(0)
    load_xin(1)
    norm_stats(0, rows17(0), xin[0], r_xin[0], junk1, r_junk1, r_ss1[0])
    norm_fin(0, 1, r_ss1[0])
    for blk in range(NB1):
        if blk + 2 < NB1:
            load_xin(blk + 2)
        if blk + 1 < NB1:
            nb1_ = blk + 1
            norm_stats(nb1_, rows17(nb1_), xin[nb1_ % NXIN], r_xin[nb1_ % NXIN], junk1, r_junk1, r_ss1[nb1_])
            norm_fin(nb1_, nb1_ + 1, r_ss1[nb1_])
        norm_one(blk, rows17(blk), xin[blk % NXIN], r_xin[blk % NXIN], gvec, r_gvec, hT_all, r_hT[blk], blk * 128,
                 hb1, r_hb1, junk1, r_junk1, stats=False, rs=r_ss1[blk])

    P.barrier(["tensor", "vector", "scalar", "sync"])
    if stop == 1:
        return finish()

    SL2 = [bfv(RING + i * 3072, 6144).rearrange("p (k w c) -> p k w c", k=16, w=3) for i in range(2)]
    o_E = RING + 6144
    o_Pf = o_E + 1536
    o_Pb = o_Pf + 4 * 256
    o_kvst = o_Pb + 4 * 256
    o_Kb = o_kvst + 3 * 256
    o_lnd = o_Kb + 4 * 64
    o_rden = o_lnd + 512
    assert o_rden + 512 <= R_END, o_rden + 512
    Etile = bfv(o_E, NE * 128).rearrange("p (e q) -> p e q", e=NE)
    r_E = Res("E")
    s_E = P.dsem("E")
    NST = 4
    STB = [0, 1, 2, 7]
    Pf = [bfv(o_Pf + i * 256, 512) for i in range(NST)]
    r_Pf = [Res(f"Pf{i}") for i in range(NST)]
    Pb = [bfv(o_Pb + i * 256, 512) for i in range(NST)]
    r_Pb = [Res(f"Pb{i}") for i in range(NST)]
    NKV = 3
    kvst = [f32v(o_kvst + i * 256, 256) for i in range(NKV)]
    r_kvst = [Res(f"kvst{i}") for i in range(NKV)]
    s_kvst = [P.dsem(f"kvst{i}") for i in range(NKV)]
    NKB = 4
    Kb = [bfv(o_Kb + i * 64, 128) for i in range(NKB)]
    r_Kb = [Res(f"Kb{i}") for i in range(NKB)]
    lnd = f32v(o_lnd, 512)
    r_lnd = Res("lnd")
    rden = f32v(o_rden, 512)
    r_rden = Res("rden")
    qT = [bfv(O_QT + g * 1056, TOK) for g in range(3)]
    KT = [bfv(O_KT + g * 1056, TOK) for g in range(3)]
    Vg = [bfv(O_V + g * 1088, 17 * 128).rearrange("p (b d) -> p b d", b=17) for g in range(3)]
    r_qT = [Res(f"qT{g}") for g in range(3)]
    r_KT = [Res(f"KT{g}") for g in range(3)]
    r_V = [Res(f"V{g}") for g in range(3)]
    qTs = bfv(O_ST, 24 * NS).rearrange("p (a t) -> p a t", a=24)
    KTs = bfv(O_ST + 768, 24 * NS).rearrange("p (a t) -> p a t", a=24)
    Vs = bfv(O_ST + 1536, 24 * 128).rearrange("p (a d) -> p a d", a=24)
    r_stash = Res("stash")
    s_out = P.dsem("out_misc")

    kvst_i = [0]
    kb_i = [0]
    slab_i = [0]
    TT5 = [(0, 512), (512, 512), (1024, 512), (1536, 512), (2048, NS)]

    for hs in range(NH):
        for g in range(3):
            si = slab_i[0] % 2
            slab_i[0] += 1
            slab = SL2[si]
            c0 = g * 1024 + hs * 128
            for w_ in range(3):
                src = wqkv[:, w_ * 3072 + c0:w_ * 3072 + c0 + 128].rearrange("(kc p) c -> p kc c", p=128)
                P.dma("gpsimd", lambda h, slab=slab, src=src, w_=w_: h.dma_start(out=slab[:, :, w_, :], in_=src),
                      slabsem[si], writes=[slabres[si]])
            for (t0, tn) in TT5:
                bk = next_bank([0, 1, 2])
                tiles_r = [r_hT[b] for b in range(t0 // 128, min(17, (t0 + tn + 127) // 128))]
                for kc in range(16):
                    mm(banks[bk][:, :tn], slab[:, kc, 0, :], hT_all[:, kc, t0:t0 + tn], kc == 0, kc == 15,
                       [slabres[si]] + tiles_r, [bres[bk]], acc=(kc > 0))
                copy_any(qT[g][:, t0:t0 + tn], banks[bk][:, :tn], [bres[bk]], [r_qT[g]])
            pend_tr = []

            def emit_ktr(g, blk, kbi, rows, t0):
                tbk = 7 if (blk // 8) % 2 == 0 else 6
                pb = bank_bf(tbk)
                j = blk % 8
                tr(pb[:, j * 128:j * 128 + rows], Kb[kbi][:rows, :], identb[:rows, :rows], [r_Kb[kbi], r_const], [bres[tbk]],
                   acc=(j > 0))
                if j == 7 or blk == NB1 - 1:
                    b0 = (blk // 8) * 8
                    ncol = t0 + rows - b0 * 128
                    copy_any(KT[g][:, b0 * 128:b0 * 128 + ncol], pb[:, :ncol], [bres[tbk]], [r_KT[g]])

            for blk in range(NB1):
                rows = rows17(blk)
                t0 = blk * 128
                bk = next_bank([0, 1, 2])
                for kc in range(16):
                    mm(banks[bk][:rows, 0:256], hT_all[:, kc, t0:t0 + rows], slab[:, kc, 1:3, :], kc == 0, kc == 15,
                       [slabres[si], r_hT[blk]], [bres[bk]], acc=(kc > 0))
                if blk == 16:
                    dst = kvs[g].rearrange("t (w h d) -> t w h d", w=2, h=NH)[:, :, hs, :]
                elif g == 2:
                    dst = kvp[2].rearrange("t (w h d) -> t w h d", w=2, h=NH)[t0:t0 + 128, :, hs, :]
                elif g == 1 and blk >= 12:
                    dst = kvp[1].rearrange("t (w h d) -> t w h d", w=2, h=NH)[t0 - 1536:t0 - 1536 + 128, :, hs, :]
                elif g == 0 and blk == 15:
                    dst = kvp[0].rearrange("t (w h d) -> t w h d", w=2, h=NH)[:, :, hs, :]
                else:
                    dst = None
                if dst is not None:
                    ki = kvst_i[0] % NKV
                    kvst_i[0] += 1
                    act(kvst[ki][:rows], banks[bk][:rows, 0:256], AF.Copy, [bres[bk]], [r_kvst[ki]])
                    P.dma("sync", lambda h, dst=dst, ki=ki, rows=rows: h.dma_start(
                        out=dst, in_=kvst[ki][:rows].rearrange("p (w d) -> p w d", w=2)), s_kvst[ki], reads=[r_kvst[ki]])
                dve(lambda h, g=g, blk=blk, bk=bk, rows=rows: h.tensor_copy(out=Vg[g][:rows, blk, :], in_=banks[bk][:rows, 128:256]),
                    [bres[bk]], [r_V[g]])
                kbi = kb_i[0] % NKB
                kb_i[0] += 1
                dve(lambda h, kbi=kbi, bk=bk, rows=rows: h.tensor_copy(out=Kb[kbi][:rows], in_=banks[bk][:rows, 0:128]),
                    [bres[bk]], [r_Kb[kbi]])
                pend_tr.append((blk, kbi, rows, t0))
                if len(pend_tr) > 2:
                    emit_ktr(g, *pend_tr.pop(0))
            while pend_tr:
                emit_ktr(g, *pend_tr.pop(0))
            a = g * NH + hs
            dve(lambda h, g=g, a=a: h.tensor_copy(out=qTs[:, a, :], in_=qT[g][:, SEQ:TOK]), [r_qT[g]], [r_stash])
            dve(lambda h, g=g, a=a: h.tensor_copy(out=KTs[:, a, :], in_=KT[g][:, SEQ:TOK]), [r_KT[g]], [r_stash])
            dve(lambda h, g=g, a=a: h.tensor_copy(out=Vs[:NS, a, :], in_=Vg[g][:NS, 16, :]), [r_V[g]], [r_stash])

        P.dma("gpsimd", lambda h, hs=hs: h.dma_start(out=Etile, in_=etab[hs].rearrange("p (e q) -> p e q", e=NE)),
              s_E, writes=[r_E])
        for c in range(4):
            nb_ = 3 + (c % 2) * 2
            db_ = nb_ + 1
            first = [True, True]
            batches = []
            for g in range(3):
                for qb in range(4 * c, 4 * c + 4):
                    dmax = min(qb, DIST[g])
                    d0 = 0
                    while d0 <= dmax:
                        n = min(4, dmax - d0 + 1)
                        batches.append((g, qb, d0, n))
                        d0 += n
            nbt = len(batches)

            def emit_st(i):
                g, qb, d0, n = batches[i]
                sb = i % NST
                bs_ = STB[sb]
                for j in range(n):
                    kb = qb - (d0 + j)
                    mm(banks[bs_][:, j * 128:(j + 1) * 128], KT[g][:, kb * 128:(kb + 1) * 128], qT[g][:, qb * 128:(qb + 1) * 128],
                       True, True, [r_KT[g], r_qT[g]], [bres[bs_]], acc=(j > 0))
                act(Pf[sb][:, :n * 128], banks[bs_][:, :n * 128], AF.Exp, [bres[bs_]], [r_Pf[sb]], scale=SCALE)
                e0 = EBASE[g] + d0
                dve(lambda h, sb=sb, n=n, e0=e0: h.tensor_tensor(
                    out=Pb[sb][:, :n * 128].rearrange("p (e q) -> p e q", e=n),
                    in0=Pf[sb][:, :n * 128].rearrange("p (e q) -> p e q", e=n),
                    in1=Etile[:, e0:e0 + n, :], op=ALU.mult),
                    [r_Pf[sb], r_E], [r_Pb[sb]])

            def emit_pv(i):
                g, qb, d0, n = batches[i]
                sb = i % NST
                ql = (qb - 4 * c) * 128
                for j in range(n):
                    kb = qb - (d0 + j)
                    mm(banks[nb_][:, ql:ql + 128], Vg[g][:, kb, :], Pb[sb][:, j * 128:(j + 1) * 128], first[0], False,
                       [r_V[g], r_Pb[sb]], [bres[nb_]], acc=not first[0], sgc=True)
                    first[0] = False
                    mm(banks[db_][:, ql:ql + 128], onesb, Pb[sb][:, j * 128:(j + 1) * 128], first[1], False,
                       [r_const, r_Pb[sb]], [bres[db_]], acc=not first[1], sgc=True)
                    first[1] = False

            LOOK = 3
            for i in range(min(LOOK, nbt)):
                emit_st(i)
            for i in range(nbt):
                if i + LOOK < nbt:
                    emit_st(i + LOOK)
                emit_pv(i)
            act(lnd, banks[db_][:], AF.Ln, [bres[db_]], [r_lnd])
            act(rden, lnd, AF.Exp, [r_lnd], [r_rden], scale=-1.0)
            dve(lambda h, nb_=nb_, hs=hs, c=c: h.tensor_tensor(out=oT[:, hs, c * 512:(c + 1) * 512], in0=banks[nb_][:], in1=rden, op=ALU.mult),
                [bres[nb_], r_rden], [r_oT[c]])

    P.barrier(list(ENGS))
    if stop == 2:
        return finish()

    o3 = O_HT
    o_enew = o3
    o_ec = o_enew + 768
    o_ktc = o_ec + 64
    o_pf3 = o_ktc + 2 * 512
    o_pb3 = o_pf3 + 2 * 512
    enew_t = bfv(o_enew, 3 * NH * NS).rearrange("p (g x) -> p g x", g=3)
    ec_t = bfv(o_ec, 2 * NH * TS).rearrange("p (k x) -> p k x", k=2)
    r_tab3 = Res("tab3")
    s_tab3 = P.dsem("tab3")
    P.dma("gpsimd", lambda h: h.dma_start(out=enew_t[:NS], in_=enew.rearrange("p (g x) -> p g x", g=3)), s_tab3, writes=[r_tab3])
    P.dma("gpsimd", lambda h: h.dma_start(out=ec_t, in_=ecin.rearrange("p (k x) -> p k x", k=2)), s_tab3, writes=[r_tab3])
    KTc = [bfv(o_ktc + i * 512, 1024).rearrange("p (a r) -> p a r", a=NH) for i in range(2)]
    r_KTc = [Res("KTc0"), Res("KTc1")]
    Pf3 = [f32v(o_pf3 + i * 512, 512) for i in range(2)]
    r_Pf3 = [Res("Pf30"), Res("Pf31")]
    Pb3 = [bfv(o_pb3 + i * 256, 512) for i in range(2)]
    r_Pb3 = [Res("Pb30"), Res("Pb31")]
    cbuf = [bfv(RING + i * 1024, 2048) for i in range(12)]
    NUMB, DENB = 3, 4
    firsts = [True, True]

    def pv_s(col0, n, lhs_v, lhs_ones, rhs, reads):
        mm(banks[NUMB][:, col0:col0 + n], lhs_v, rhs, firsts[0], False, reads, [bres[NUMB]], acc=not firsts[0], sgc=True)
        firsts[0] = False
        mm(banks[DENB][:, col0:col0 + n], lhs_ones, rhs, firsts[1], False, reads + [r_const], [bres[DENB]], acc=not firsts[1], sgc=True)
        firsts[1] = False

    for g in range(3):
        sb = g % 2
        for hh in range(NH):
            a = g * NH + hh
            mm(banks[sb][:NS, hh * NS:(hh + 1) * NS], KTs[:, a, :], qTs[:, a, :], True, True, [r_stash], [bres[sb]], acc=(hh > 0))
        act(Pf3[sb][:NS, :], banks[sb][:NS, :], AF.Exp, [bres[sb]], [r_Pf3[sb]], scale=SCALE)
        dve(lambda h, sb=sb, g=g: h.tensor_tensor(out=Pb3[sb][:NS, :], in0=Pf3[sb][:NS, :], in1=enew_t[:NS, g, :], op=ALU.mult),
            [r_Pf3[sb], r_tab3], [r_Pb3[sb]])
        for hh in range(NH):
            a = g * NH + hh
            pv_s(hh * NS, NS, Vs[:NS, a, :], onesb[:NS, :], Pb3[sb][:NS, hh * NS:(hh + 1) * NS], [r_stash, r_Pb3[sb]])

    cb_i = [0]
    units = [(b, g) for b in range(NSEQ_S) for g in range(3)]

    def load_unit(u):
        b, g = units[u]
        ids = []
        nblk = 1 if g == 0 else TS
        for t in range(nblk):
            ci = cb_i[0] % 12
            cb_i[0] += 1
            if g == 0:
                src = cache[0][b]
            else:
                src = cache[g][b].rearrange("(j r) n -> r j n", r=DIL[g])[t]
            P.dma("gpsimd", lambda h, ci=ci, src=src: h.dma_start(out=cbuf[ci], in_=src), cbsem[ci], writes=[cbres[ci]])
            ids.append(ci)
        return ids

    pending = {0: load_unit(0), 1: load_unit(1)}
    kt_i = [0]
    for u, (b, g) in enumerate(units):
        if u + 2 < len(units):
            pending[u + 2] = load_unit(u + 2)
        ids = pending.pop(u)
        sb = u % 2
        nq = TS if g == 0 else 1
        for t, ci in enumerate(ids):
            ki = kt_i[0] % 2
            kt_i[0] += 1
            tb = 6 + ki
            pbk = bank_bf(tb)
            for hh in range(NH):
                tr(pbk[:, hh * 128:(hh + 1) * 128], cbuf[ci][:, hh * 128:(hh + 1) * 128], identb, [cbres[ci], r_const], [bres[tb]], acc=(hh > 0))
            copy_any(KTc[ki], pbk.rearrange("p (a r) -> p a r", a=NH), [bres[tb]], [r_KTc[ki]])
            for hh in range(NH):
                a = g * NH + hh
                if g == 0:
                    mm(banks[sb][:, hh * TS:(hh + 1) * TS], KTc[ki][:, hh, :], qTs[:, a, b * TS:(b + 1) * TS], True, True,
                       [r_KTc[ki], r_stash], [bres[sb]], acc=(hh > 0 or t > 0))
                else:
                    mm(banks[sb][:, hh * TS + t:hh * TS + t + 1], KTc[ki][:, hh, :], qTs[:, a, b * TS + t:b * TS + t + 1], True, True,
                       [r_KTc[ki], r_stash], [bres[sb]], acc=(hh > 0 or t > 0))
        act(Pf3[sb][:, :NH * TS], banks[sb][:, :NH * TS], AF.Exp, [bres[sb]], [r_Pf3[sb]], scale=SCALE)
        kind = 0 if g == 0 else 1
        dve(lambda h, sb=sb, kind=kind: h.tensor_tensor(out=Pb3[sb][:, :NH * TS], in0=Pf3[sb][:, :NH * TS], in1=ec_t[:, kind, :], op=ALU.mult),
            [r_Pf3[sb], r_tab3], [r_Pb3[sb]])
        for t, ci in enumerate(ids):
            for hh in range(NH):
                vv = cbuf[ci][:, 1024 + hh * 128:1024 + (hh + 1) * 128]
                if g == 0:
                    pv_s(hh * NS + b * TS, TS, vv, onesb, Pb3[sb][:, hh * TS:(hh + 1) * TS], [cbres[ci], r_Pb3[sb]])
                else:
                    pv_s(hh * NS + b * TS + t, 1, vv, onesb, Pb3[sb][:, hh * TS + t:hh * TS + t + 1], [cbres[ci], r_Pb3[sb]])
    dve(lambda h: h.reciprocal(out=rden, in_=banks[DENB][:]), [bres[DENB]], [r_rden])
    dve(lambda h: h.tensor_tensor(out=oT[:, :, SEQ:TOK], in0=banks[NUMB][:].rearrange("p (a t) -> p a t", a=NH),
                                  in1=rden.rearrange("p (a t) -> p a t", a=NH), op=ALU.mult),
        [bres[NUMB], r_rden], [r_oT[4]])

    s_oTd = P.dsem("oTd")
    P.dma("sync", lambda h: h.dma_start(out=oTd, in_=bfv(O_OT, 8 * TOK)), s_oTd, reads=r_oT)
    P.barrier(list(ENGS))
    if stop == 3:
        return finish()

    SL = [bfv(RING + i * 4096, 8192) for i in range(3)]
    slab_i[0] = 0

    def next_slab():
        si = slab_i[0] % 3
        slab_i[0] += 1
        return si

    O_X = R_END
    O_H = O_X + 9 * 2048
    O_A = O_H + 8704
    O_T = O_A + 4352
    assert O_T + 7680 <= ARENA_WORDS
    xres_ = f32v(O_X, 9 * 2048).rearrange("p (b d) -> p b d", b=9)
    s_xld = P.dsem("xld")
    xb_sems = [P.dsem(f"xb{i}") for i in range(9)]
    s_gv = P.dsem("gv")
    s_y = [P.dsem("y0"), P.dsem("y1")]
    s_park = P.dsem("park")
    s_stq = [P.dsem("stq0"), P.dsem("stq1")]
    s_co = P.dsem("convout")
    s_cv = P.dsem("cvin")

    o_cvl = O_T
    r_cv = Res("cvec")
    r_cvl = Res("cvl")
    cvl_t = f32v(O_T, 640).rearrange("p (a c) -> p a c", a=5)
    for a_ in range(5):
        nr = min(128, 576 - a_ * 128)
        P.dma("sync", lambda h, a_=a_, nr=nr: h.dma_start(out=cvl_t[:nr, a_, :], in_=cvin[a_ * 128:a_ * 128 + nr, :]), s_cv, writes=[r_cvl])
    for a_ in range(5):
        nr = min(128, 576 - a_ * 128)
        bk = next_bank([0, 1, 2, 3, 4, 5])
        tr(banks[bk][:, :nr], cvl_t[:nr, a_, :], identf[:nr, :nr], [r_cvl, r_const], [bres[bk]])
        dve(lambda h, a_=a_, nr=nr, bk=bk: h.tensor_copy(out=cvec[:, a_ * 128:a_ * 128 + nr], in_=banks[bk][:, :nr]), [bres[bk]], [r_cv])
    CV_BPW1, CV_BDW, CV_LNG, CV_LNB, CV_WDW = 0, 32, 48, 64, 80
    P.barrier(["tensor", "vector", "scalar", "sync"])

    ALLB = [0, 1, 2, 3, 4, 5, 6, 7]
    r_x_all = [Res(f"x{b}") for b in range(9)]

    def load_x_block(pass_, blk):
        rows = NS if (pass_ == 1 and blk == 8) else 128
        src = xs if (pass_ == 1 and blk == 8) else xp[pass_ * 1024 + blk * 128:pass_ * 1024 + (blk + 1) * 128, :]
        P.dma("sync", lambda h: h.dma_start(out=xres_[:rows, blk, :], in_=src), xb_sems[blk], writes=[r_x_all[blk]])

    for ps_ in range(2):
        nblk = 8 if ps_ == 0 else 9
        tokbase = 0 if ps_ == 0 else 1024
        T = 1024 if ps_ == 0 else 1024 + NS

        def rows_p(blk):
            return NS if (ps_ == 1 and blk == 8) else 128

        tiles = [(0, 512), (512, 512)] + ([(1024, NS)] if ps_ == 1 else [])
        r_x = r_x_all
        hT = bfv(O_H, 16 * 1088).rearrange("p (c t) -> p c t", c=16)
        r_h = [Res(f"h{b}") for b in range(nblk)]
        abuf = [bfv(O_A + i * 2176, 4 * 1088).rearrange("p (m t) -> p m t", m=4) for i in range(2)]
        r_a = [Res("a0"), Res("a1")]

        s_xb = xb_sems
        if ps_ == 0:
            for blk in range(nblk):
                load_x_block(0, blk)
        oTp = bfv(O_A, 8 * 1088).rearrange("p (s t) -> p s t", s=8)
        r_oTp = Res("oTp")
        P.dma("sync", lambda h, tb=tokbase, T=T: h.dma_start(
            out=oTp[:, :, :T], in_=oTd.rearrange("p (s t) -> p s t", s=8)[:, :, tb:tb + T]), s_oTd, writes=[r_oTp])
        wo_sl = []
        for e2 in range(2):
            si = next_slab()
            wv = SL[si].rearrange("p (e s n) -> p e s n", e=2, s=8)
            for e_ in range(2):
                dsl = e2 * 2 + e_
                P.dma("gpsimd", lambda h, wv=wv, e_=e_, dsl=dsl: h.dma_start(
                    out=wv[:, e_, :, :], in_=wo[:, dsl * 512:(dsl + 1) * 512].rearrange("(s p) n -> p s n", p=128)),
                    slabsem[si], writes=[slabres[si]])
                wo_sl.append((wv[:, e_, :, :], slabres[si]))
        for blk in range(nblk):
            rows = rows_p(blk)
            for dsl in range(4):
                wv_, wres = wo_sl[dsl]
                bk = next_bank(ALLB)
                for s_ in range(NH):
                    mm(banks[bk][:rows, :], oTp[:, s_, blk * 128:blk * 128 + rows], wv_[:, s_, :], s_ == 0, s_ == NH - 1,
                       [r_oTp, wres], [bres[bk]], acc=(s_ > 0))
                xv = xres_[:rows, blk, dsl * 512:(dsl + 1) * 512]
                dve(lambda h, xv=xv, bk=bk, rows=rows: h.tensor_tensor(out=xv, in0=xv, in1=banks[bk][:rows, :], op=ALU.add),
                    [bres[bk], r_x[blk]], [r_x[blk]])
        P.barrier(["tensor", "vector", "scalar", "sync"])

        def do_norm(row):
            hbs = [bfv(O_T + i * 1024, 2048) for i in range(2)]
            rhbs = [Res("hb0"), Res("hb1")]
            junk = bfv(O_T + 2048, 2048)
            rjunk = Res("junk")
            gv = f32v(O_T + 3072, 2048)
            rgv = Res("gv")
            load_gvec(gv, row, rgv, s_gv)
            dve(lambda h: h.memset(ssv, 0.0), [r_ss], [r_ss])
            rsa, rsb = Res("ssa"), Res("ssb")
            rsa.w = r_ss.w
            rsb.w = r_ss.w
            groups = [(0, 4, rsa), (4, nblk, rsb)]
            for (b0, b1, rs_) in groups:
                for blk in range(b0, b1):
                    norm_stats(blk, rows_p(blk), xres_[:, blk, :], r_x[blk], junk, rjunk, rs_)
                norm_fin(b0, b1, rs_)
            for (b0, b1, rs_) in groups:
                for blk in range(b0, b1):
                    norm_one(blk, rows_p(blk), xres_[:, blk, :], r_x[blk], gv, rgv, hT, r_h[blk], blk * 128, hbs, rhbs, junk, rjunk,
                             stats=False, rs=rs_)
            dve(lambda h: h.memset(f32v(o_misc + 8, 1), 0.0), [rsa, rsb], [r_ss])

        def blocks_of(t0, tn):
            return [r_h[b] for b in range(t0 // 128, (t0 + tn + 127) // 128)]

        def do_mlp(i):
            rt = [f32v(O_T + 5120 + k * 512, 512) for k in range(3)]
            r_rt = [Res(f"rt{k}") for k in range(3)]
            rt_i = [0]

            def up(s):
                si = next_slab()
                wv = SL[si].rearrange("p (k n) -> p k n", k=16)
                P.dma("gpsimd", lambda h, wv=wv, s=s: h.dma_start(
                    out=wv, in_=wup[i][:, s * 512:(s + 1) * 512].rearrange("(kc p) n -> p kc n", p=128)), slabsem[si], writes=[slabres[si]])
                ab = abuf[s % 2]
                for m in range(4):
                    for (t0, tn) in tiles:
                        bk = next_bank(ALLB)
                        rr = blocks_of(t0, tn)
                        for kc in range(16):
                            mm(banks[bk][:, :tn], wv[:, kc, m * 128:(m + 1) * 128], hT[:, kc, t0:t0 + tn], kc == 0, kc == 15,
                               [slabres[si]] + rr, [bres[bk]], acc=(kc > 0))
                        k = rt_i[0] % 3
                        rt_i[0] += 1
                        act(rt[k][:, :tn], banks[bk][:, :tn], AF.Relu, [bres[bk]], [r_rt[k]])
                        act(ab[:, m, t0:t0 + tn], rt[k][:, :tn], AF.Square, [r_rt[k]], [r_a[s % 2]])

            def down(s):
                si = next_slab()
                wv = SL[si].rearrange("p (m n) -> p m n", m=4)
                P.dma("gpsimd", lambda h, wv=wv, s=s: h.dma_start(
                    out=wv, in_=wdown[i][s * 512:(s + 1) * 512, :].rearrange("(m p) n -> p m n", p=128)), slabsem[si], writes=[slabres[si]])
                ab = abuf[s % 2]
                for blk in range(nblk):
                    rows = rows_p(blk)
                    for dsl in range(4):
                        bk = next_bank(ALLB)
                        for m in range(4):
                            mm(banks[bk][:rows, :], ab[:, m, blk * 128:blk * 128 + rows], wv[:, m, dsl * 512:(dsl + 1) * 512],
                               m == 0, m == 3, [r_a[s % 2], slabres[si]], [bres[bk]], acc=(m > 0))
                        xv = xres_[:rows, blk, dsl * 512:(dsl + 1) * 512]
                        dve(lambda h, xv=xv, bk=bk, rows=rows: h.tensor_tensor(out=xv, in0=xv, in1=banks[bk][:rows, :], op=ALU.add),
                            [bres[bk], r_x[blk]], [r_x[blk]])

            up(0)
            for s in range(16):
                if s + 1 < 16:
                    up(s + 1)
                down(s)

        do_norm(1)
        do_mlp(0)
        P.barrier(["tensor", "vector", "scalar", "sync"])

        do_norm(2)
        for blk in range(nblk):
            rows = rows_p(blk)
            P.dma("sync", lambda h, blk=blk, rows=rows: h.dma_start(out=xpark[blk * 128:blk * 128 + rows, :], in_=xres_[:rows, blk, :]),
                  s_park, reads=[r_x[blk]])
        P.barrier(["tensor", "vector", "scalar", "sync"])
        zT = f32v(O_X, 16 * 1088).rearrange("p (c t) -> p c t", c=16)
        r_z = [Res(f"z{j}") for j in range(16)]
        XW = 1056
        o_ext = O_A
        NEXT = 2
        ext = [f32v(o_ext + i * XW, XW) for i in range(NEXT)]
        r_ext = [Res(f"ext{i}") for i in range(NEXT)]
        o_exb = o_ext + NEXT * XW
        exb = [bfv(o_exb + i * 528, 1056) for i in range(2)]
        r_exb = [Res("exb0"), Res("exb1")]
        o_dg = o_exb + 2 * 528
        dg = bfv(o_dg, CW * 128).rearrange("p (k c) -> p k c", k=CW)
        r_dg = [Res(f"dg{k}") for k in range(CW)]
        o_sig = o_dg + 1984
        sig = [f32v(o_sig, 512) for i in range(2)]
        r_sig = [Res("sig0")] * 2
        o_exts = o_sig + 512
        exts = [f32v(o_exts + i * 544, 544).rearrange("p (b k) -> p b k", b=NSEQ_S) for i in range(2)]
        r_exts = [Res("exts0"), Res("exts1")]
        o_stq = o_exts + 1088
        stq = [f32v(o_stq + i * 512, 512).rearrange("p (a c) -> p a c", a=4) for i in range(2)]
        r_stq = [Res("stq0"), Res("stq1")]
        o_cpo = o_stq + 1024
        cpo = f32v(o_cpo, 2048)
        r_cpo = Res("cpo")
        cso = f32v(o_cpo + 2048, 2048)
        r_cso = Res("cso")
        usc = [f32v(o_cpo + 4096 + i * 64, 64) for i in range(2)]
        r_usc = [Res("usc0"), Res("usc1")]
        assert o_cpo + 4096 + 128 <= ARENA_WORDS
        sig_i = [0]
        if ps_ == 1:
            P.dma("sync", lambda h: h.dma_start(out=convs[:, 0:26, :], in_=sconv[:, 4:30, :]), s_co)
        for sl in range(8):
            si = next_slab()
            wv = SL[si].rearrange("p (k w c) -> p k w c", k=16, w=2)
            for w_ in range(2):
                P.dma("gpsimd", lambda h, wv=wv, sl=sl, w_=w_: h.dma_start(
                    out=wv[:, :, w_, :], in_=wpw1[:, w_ * D + sl * 256:w_ * D + (sl + 1) * 256].rearrange("(kc p) c -> p kc c", p=128)),
                    slabsem[si], writes=[slabres[si]])
            for jj in range(2):
                j = sl * 2 + jj
                ei = j % NEXT
                ex = ext[ei]
                rex = r_ext[ei]
                if ps_ == 0:
                    dve(lambda h, ex=ex: h.memset(ex[:, 0:30], 0.0), [], [rex])
                else:
                    dve(lambda h, ex=ex, j=j: h.tensor_copy(out=ex[:, 0:30], in_=utail[:, j, :]), [r_utail], [rex])
                for (t0, tn) in tiles:
                    bka = next_bank(ALLB)
                    bkb = next_bank(ALLB)
                    rr = blocks_of(t0, tn)
                    for kc in range(16):
                        mm(banks[bka][:, :tn], wv[:, kc, 0, jj * 128:(jj + 1) * 128], hT[:, kc, t0:t0 + tn], kc == 0, kc == 15,
                           [slabres[si]] + rr, [bres[bka]], acc=(kc > 0))
                    for kc in range(16):
                        mm(banks[bkb][:, :tn], wv[:, kc, 1, jj * 128:(jj + 1) * 128], hT[:, kc, t0:t0 + tn], kc == 0, kc == 15,
                           [slabres[si]] + rr, [bres[bkb]], acc=(kc > 0))
                    k = sig_i[0] % 2
                    sig_i[0] += 1
                    act(sig[k][:, :tn], banks[bkb][:, :tn], AF.Sigmoid, [bres[bkb], r_cv], [r_sig[k]],
                        bias=cvec[:, CV_BPW1 + 16 + j:CV_BPW1 + 16 + j + 1])
                    if t0 < 1024:
                        uo = ex[:, 30 + t0:30 + t0 + tn]
                        wr = [rex]
                    else:
                        uo = usc[j % 2]
                        wr = [r_usc[j % 2]]
                    if t0 < 1024:
                        dve(lambda h, uo=uo, bka=bka, tn=tn, k=k, j=j: h.scalar_tensor_tensor(
                            out=uo, in0=banks[bka][:, :tn], scalar=cvec[:, CV_BPW1 + j:CV_BPW1 + j + 1], in1=sig[k][:, :tn],
                            op0=ALU.add, op1=ALU.mult), [bres[bka], r_sig[k], r_cv], wr)
                    else:
                        dve(lambda h, uo=uo, bka=bka, tn=tn, k=k, j=j: h.scalar_tensor_tensor(
                            out=uo, in0=banks[bka][:, :tn], scalar=cvec[:, CV_BPW1 + j:CV_BPW1 + j + 1], in1=sig[k][:, :tn],
                            op0=ALU.add, op1=ALU.mult), [bres[bka], r_sig[k], r_cv], wr)
                zj = zT[:, j, 0:1024]
                if PE_CONV:
                    xb_ = exb[j % 2]
                    rxb = r_exb[j % 2]
                    act(xb_[:, 0:1054], ex[:, 0:1054], AF.Copy, [rex], [rxb])
                    for k_ in range(CW):
                        col = CV_WDW + k_ * 16 + j
                        act(dg[:, k_, :], identf, AF.Copy, [r_const, r_cv], [r_dg[k_]], scale=cvec[:, col:col + 1])
                    for (t0, tn) in tiles[:2]:
                        bk = next_bank(ALLB)
                        for k_ in range(CW):
                            mm(banks[bk][:, :512], dg[:, k_, :], xb_[:, k_ + t0:k_ + t0 + 512], k_ == 0, k_ == CW - 1,
                               [r_dg[k_], rxb], [bres[bk]], acc=(k_ > 0))
                        act(zT[:, j, t0:t0 + 512], banks[bk][:, :512], AF.Identity, [bres[bk], r_cv], [r_z[j]],
                            bias=cvec[:, CV_BDW + j:CV_BDW + j + 1])
                ceng = "gpsimd" if j in POOL_CHUNKS else "vector"
                if not PE_CONV:
                    P.op(ceng, lambda h, zj=zj, ex=ex, j=j: h.tensor_scalar(
                        out=zj, in0=ex[:, 0:1024], scalar1=cvec[:, CV_WDW + j:CV_WDW + j + 1], scalar2=cvec[:, CV_BDW + j:CV_BDW + j + 1],
                        op0=ALU.mult, op1=ALU.add), reads=[rex, r_cv], writes=[r_z[j]])
                for k_ in range(1, CW if not PE_CONV else 1):
                    P.op(ceng, lambda h, zj=zj, ex=ex, j=j, k_=k_: h.scalar_tensor_tensor(
                        out=zj, in0=ex[:, k_:k_ + 1024], scalar=cvec[:, CV_WDW + k_ * 16 + j:CV_WDW + k_ * 16 + j + 1], in1=zj,
                        op0=ALU.mult, op1=ALU.add), reads=[rex, r_cv, r_z[j]], writes=[r_z[j]])
                if ps_ == 0:
                    dve(lambda h, ex=ex, j=j: h.tensor_copy(out=utail[:, j, :], in_=ex[:, 1024:1054]), [rex], [r_utail])
                else:
                    bk = next_bank(ALLB)
                    tr(banks[bk][:30, 0:128], ex[:, 1024:1054], identf, [rex, r_const], [bres[bk]])
                    copy_any(cpo[:30, j * 128:(j + 1) * 128], banks[bk][:30, 0:128], [bres[bk]], [r_cpo])
                    qi = j % 2
                    P.dma("sync", lambda h, qi=qi, j=j: h.dma_start(
                        out=stq[qi][:120], in_=sconv[:, :, j * 128:(j + 1) * 128].rearrange("(a b) k c -> (b k) a c", a=4)),
                        s_stq[qi], writes=[r_stq[qi]])
                    bk = next_bank(ALLB)
                    for a_ in range(4):
                        tr(banks[bk][:, a_ * 120:(a_ + 1) * 120], stq[qi][:120, a_, :], identf[:120, :120], [r_stq[qi], r_const], [bres[bk]],
                           acc=(a_ > 0))
                    exj = exts[j % 2]
                    rexs = r_exts[j % 2]
                    copy_any(exj[:, :, 0:30], banks[bk][:, 0:480].rearrange("p (b k) -> p b k", b=NSEQ_S), [bres[bk]], [rexs])
                    dve(lambda h, exj=exj, j=j: h.tensor_copy(out=exj[:, :, 30:34], in_=usc[j % 2].rearrange("p (b t) -> p b t", b=NSEQ_S)),
                        [r_usc[j % 2], rexs], [rexs])
                    zs = zT[:, j, 1024:1088].rearrange("p (b t) -> p b t", b=NSEQ_S)
                    dve(lambda h, zs=zs, exj=exj, j=j: h.tensor_scalar(
                        out=zs, in0=exj[:, :, 0:4], scalar1=cvec[:, CV_WDW + j:CV_WDW + j + 1], scalar2=cvec[:, CV_BDW + j:CV_BDW + j + 1],
                        op0=ALU.mult, op1=ALU.add), [rexs, r_cv], [r_z[j]])
                    for k_ in range(1, CW):
                        dve(lambda h, zs=zs, exj=exj, j=j, k_=k_: h.scalar_tensor_tensor(
                            out=zs, in0=exj[:, :, k_:k_ + 4], scalar=cvec[:, CV_WDW + k_ * 16 + j:CV_WDW + k_ * 16 + j + 1], in1=zs,
                            op0=ALU.mult, op1=ALU.add), [rexs, r_cv, r_z[j]], [r_z[j]])
                    bk = next_bank(ALLB)
                    tr(banks[bk][:NS, 0:128], usc[j % 2], identf, [r_usc[j % 2], r_const], [bres[bk]])
                    copy_any(cso[:NS, j * 128:(j + 1) * 128], banks[bk][:NS, 0:128], [bres[bk]], [r_cso])
        if ps_ == 1:
            P.dma("sync", lambda h: h.dma_start(out=convp, in_=cpo[:30, :]), s_co, reads=[r_cpo])
            for b_ in range(NSEQ_S):
                P.dma("sync", lambda h, b_=b_: h.dma_start(out=convs[b_, 26:30, :], in_=cso[b_ * TS:(b_ + 1) * TS, :]), s_co, reads=[r_cso])
        P.barrier(["tensor", "vector", "scalar", "sync"])

        onesf = f32v(O_A, 128)
        r_onesf = Res("onesf")
        dve(lambda h: h.memset(onesf, 1.0 / D), [], [r_onesf])
        o_sq = O_A + 128
        sq = [f32v(o_sq + i * 1088, 1088) for i in range(2)]
        r_sq = [Res("sq0"), Res("sq1")]
        rsd = f32v(o_sq + 2176, 1088)
        r_rsd = Res("rsd")
        assert o_sq + 3 * 1088 <= O_T
        mb = [0, 1, 2]
        vb = [3, 4, 5]
        for ti, (t0, tn) in enumerate(tiles):
            for j in range(16):
                mm(banks[mb[ti]][:, :tn], onesf, zT[:, j, t0:t0 + tn], j == 0, j == 15, [r_onesf, r_z[j]], [bres[mb[ti]]], acc=(j > 0))
        for j in range(16):
            for ti, (t0, tn) in enumerate(tiles):
                zv = zT[:, j, t0:t0 + tn]
                dve(lambda h, zv=zv, ti=ti, tn=tn: h.tensor_tensor(out=zv, in0=zv, in1=banks[mb[ti]][:, :tn], op=ALU.subtract),
                    [bres[mb[ti]], r_z[j]], [r_z[j]])
            act(sq[j % 2][:, :T], zT[:, j, :T], AF.Square, [r_z[j]], [r_sq[j % 2]])
            for ti, (t0, tn) in enumerate(tiles):
                mm(banks[vb[ti]][:, :tn], onesf, sq[j % 2][:, t0:t0 + tn], j == 0, j == 15, [r_onesf, r_sq[j % 2]], [bres[vb[ti]]], acc=(j > 0))
        for ti, (t0, tn) in enumerate(tiles):
            act(rsd[:, t0:t0 + tn], banks[vb[ti]][:, :tn], AF.Sqrt, [bres[vb[ti]], r_const], [r_rsd], bias=epsl)
            dve(lambda h, t0=t0, tn=tn: h.reciprocal(out=rsd[:, t0:t0 + tn], in_=rsd[:, t0:t0 + tn]), [r_rsd], [r_rsd])
        sT = hT
        for j in range(16):
            dve(lambda h, j=j, T=T: h.tensor_tensor(out=zT[:, j, :T], in0=zT[:, j, :T], in1=rsd[:, :T], op=ALU.mult), [r_z[j], r_rsd], [r_z[j]])
            act(sT[:, j, :T], zT[:, j, :T], AF.Silu, [r_z[j], r_cv], [r_h[b] for b in range(nblk)],
                scale=cvec[:, CV_LNG + j:CV_LNG + j + 1], bias=cvec[:, CV_LNB + j:CV_LNB + j + 1])
        P.barrier(["tensor", "vector", "scalar", "sync"])

        bb = f32v(O_T, 2048)
        r_bb = Res("bb")
        load_gvec(bb, 5, r_bb, s_gv)
        for blk in range(nblk):
            rows = rows_p(blk)
            P.dma("sync", lambda h, blk=blk, rows=rows: h.dma_start(out=xres_[:rows, blk, :], in_=xpark[blk * 128:blk * 128 + rows, :]),
                  s_xld, writes=[r_x[blk]])
        for blk in range(nblk):
            r_x[blk].w = (s_xld[0], s_xld[1], "dma")
        for blk in range(nblk):
            rows = rows_p(blk)
            dve(lambda h, blk=blk, rows=rows: h.tensor_tensor(out=xres_[:rows, blk, :], in0=xres_[:rows, blk, :], in1=bb[:rows], op=ALU.add),
                [r_x[blk], r_bb], [r_x[blk]])
        for s in range(4):
            si = next_slab()
            wv = SL[si].rearrange("p (m n) -> p m n", m=4)
            P.dma("gpsimd", lambda h, wv=wv, s=s: h.dma_start(
                out=wv, in_=wpw2[s * 512:(s + 1) * 512, :].rearrange("(m p) n -> p m n", p=128)), slabsem[si], writes=[slabres[si]])
            for blk in range(nblk):
                rows = rows_p(blk)
                for dsl in range(4):
                    bk = next_bank(ALLB)
                    for m in range(4):
                        mm(banks[bk][:rows, :], sT[:, s * 4 + m, blk * 128:blk * 128 + rows], wv[:, m, dsl * 512:(dsl + 1) * 512],
                           m == 0, m == 3, [r_h[blk], slabres[si]], [bres[bk]], acc=(m > 0))
                    xv = xres_[:rows, blk, dsl * 512:(dsl + 1) * 512]
                    dve(lambda h, xv=xv, bk=bk, rows=rows: h.tensor_tensor(out=xv, in0=xv, in1=banks[bk][:rows, :], op=ALU.add),
                        [bres[bk], r_x[blk]], [r_x[blk]])
        P.barrier(["tensor", "vector", "scalar", "sync"])

        do_norm(3)
        do_mlp(1)
        P.barrier(["tensor", "vector", "scalar", "sync"])

        gv = f32v(O_T, 2048)
        rgv = Res("gvf")
        load_gvec(gv, 4, rgv, s_gv)
        yst = [f32v(O_T + 2048 + i * 2048, 2048) for i in range(2)]
        r_yst = [Res("yst0"), Res("yst1")]
        junk = bfv(O_T + 6144, 2048)
        rjunk = Res("junkf")
        dve(lambda h: h.memset(ssv, 0.0), [r_ss], [r_ss])
        for blk in range(nblk):
            norm_stats(blk, rows_p(blk), xres_[:, blk, :], r_x[blk], junk, rjunk)
        norm_fin(0, nblk)
        for blk in range(nblk):
            rows = rows_p(blk)
            xb = xres_[:, blk, :]
            col = blk
            yi = blk % 2
            dve(lambda h, rows=rows, col=col, xb=xb, yi=yi: h.scalar_tensor_tensor(
                out=yst[yi][:rows], in0=xb[:rows], scalar=rstdv[:rows, col:col + 1], in1=gv[:rows], op0=ALU.mult, op1=ALU.mult),
                [r_x[blk], r_ss, rgv], [r_yst[yi]])
            if ps_ == 1 and blk == 8:
                dst = ys
            else:
                dst = yp[tokbase + blk * 128:tokbase + (blk + 1) * 128, :]
            P.dma("sync", lambda h, dst=dst, yi=yi, rows=rows: h.dma_start(out=dst, in_=yst[yi][:rows]), s_y[yi], reads=[r_yst[yi]])
            if ps_ == 0:
                load_x_block(1, blk)
                if blk == nblk - 1:
                    load_x_block(1, 8)
        P.barrier(["tensor", "vector", "scalar", "sync"], skip=s_y)

    return finish()


_CACHE = {}


def _prep_inputs(inp):
    etab, enew, ec, ident = _tables()
    f = lambda a: np.ascontiguousarray(np.asarray(a, dtype=np.float32))
    nvec = f(np.stack([inp["attn_norm"][0], inp["mlp_norm"][0], inp["conv_norm"][0], inp["mlp_norm"][1],
                       inp["final_norm"], inp["b_pw2"][0]], axis=0))
    cvin = f(np.concatenate([np.asarray(inp["b_pw1"][0]).reshape(32, 128), np.asarray(inp["b_dw"][0]).reshape(16, 128),
                             np.asarray(inp["conv_ln_g"][0]).reshape(16, 128), np.asarray(inp["conv_ln_b"][0]).reshape(16, 128),
                             np.asarray(inp["w_dw"][0]).reshape(CW * 16, 128)], axis=0))
    shared = dict(wqkv=f(inp["w_qkv"][0]), wo=f(inp["w_o"][0]), wpw1=f(inp["w_pw1"][0]), wpw2=f(inp["w_pw2"][0]),
                  wup=f(inp["w_up"]), wdown=f(inp["w_down"]), nvec=nvec, cvin=cvin, etab=etab, enew=enew, ecin=ec, ident=ident)
    maps = []
    for c in range(NCORES):
        sl = slice(c * NSEQ_S, (c + 1) * NSEQ_S)
        m = dict(shared)
        m["xp"] = f(inp["x_prompt"][c])
        m["xs"] = f(np.asarray(inp["x_sample"][sl]).reshape(NS, D))
        m["c0"] = f(np.asarray(inp["cache_kv_g0"][0, sl]).reshape(NSEQ_S, 128, 2048))
        m["c1"] = f(np.asarray(inp["cache_kv_g1"][0, sl]).reshape(NSEQ_S, 512, 2048))
        m["c2"] = f(np.asarray(inp["cache_kv_g2"][0, sl]).reshape(NSEQ_S, 2048, 2048))
        m["sconv"] = f(inp["state_conv"][0, sl])
        maps.append(m)
    return maps


def kernel(**inputs):
    if "nc" not in _CACHE:
        _CACHE["nc"] = build_program()[0]
    nc = _CACHE["nc"]
    maps = _prep_inputs(inputs)
    res = run_bass_kernel_spmd(nc, maps, core_ids=list(range(NCORES)))
    R = res.results
    cat = lambda k: np.stack([np.asarray(r[k], dtype=np.float32) for r in R], axis=0)
    y_prompt = cat("yp")
    y_sample = cat("ys").reshape(NCORES * NSEQ_S, TS, D)
    kvp_o = [cat(f"kv{g}p").reshape(NCORES, -1, 2, NH, HD)[None] for g in range(3)]
    conv_p = cat("convp")[None]
    kvs_o = [cat(f"kv{g}s").reshape(NCORES * NSEQ_S, TS, 2, NH, HD)[None] for g in range(3)]
    conv_s = cat("convs").reshape(NCORES * NSEQ_S, CW - 1, D)[None]
    return (y_prompt, y_sample, kvp_o[0], kvp_o[1], kvp_o[2], conv_p, kvs_o[0], kvs_o[1], kvs_o[2], conv_s)
```
